# Optimizing a Trainium2 kernel written in Bass

```python
import math
import jax, jax.numpy as jnp
from jax import lax
import numpy as np

D_MODEL = 2048
BATCH = 8
SEQ = 2048
DEPTH = 2

D_FF = 5632
NORM_EPS = 1e-6
NEG_INF = -1e30
FORCE = 1e6
LOG_FLOOR = 1e-30

GDN_HEADS = 8
GDN_DK = 128
GDN_DV = 128
GDN_CONV = 4
GDN_CHUNK = 64
GDN_QK = GDN_HEADS * GDN_DK
GDN_V = GDN_HEADS * GDN_DV

NSA_HEADS = 8
NSA_GROUPS = 2
NSA_DH = 128
NSA_Q = NSA_HEADS * NSA_DH
NSA_KV = NSA_GROUPS * NSA_DH
CMP_LEN = 32
CMP_STRIDE = 16
SLC_LEN = 64
SLC_TOPN = 8
WIN_LEN = 512
WIN_Q_BLOCK = 128
SLC_Q_BLOCK = 64

HG_HEADS = 8
HG_DK = 128
HG_DV = 128
HG_CHUNK = 64
HG_K = HG_HEADS * HG_DK
HG_V = HG_HEADS * HG_DV

N_BRANCH = 3
IN_WIDTHS = (2 * GDN_QK + GDN_V, GDN_HEADS, GDN_HEADS, GDN_V,
             NSA_Q, 6 * NSA_KV, 3 * NSA_HEADS,
             HG_K, HG_K, HG_V, HG_V,
             N_BRANCH * D_MODEL)
N_IN = sum(IN_WIDTHS)

kernel_name = 'hybrid_gdn_nsa_hgrn2_macaron'


def rmsnorm(x, g, eps=NORM_EPS):
    xf = x.astype(jnp.float32)
    y = xf * lax.rsqrt(jnp.mean(xf * xf, axis=-1, keepdims=True) + eps)
    return (y * g.astype(jnp.float32)).astype(x.dtype)


def l2norm(x, eps=1e-6):
    return x * lax.rsqrt(jnp.sum(x * x, axis=-1, keepdims=True) + eps)


def swiglu(h, wg, wu, wd):
    return (jax.nn.silu(h @ wg) * (h @ wu)) @ wd


def split_cols(u, widths):
    offs = np.cumsum(np.array(widths))[:-1].tolist()
    return jnp.split(u, offs, axis=-1)


def causal_dwconv(x, w):
    k, c = w.shape
    return lax.conv_general_dilated(
        x, w.astype(x.dtype)[:, None, :], window_strides=(1,), padding=[(k - 1, 0)],
        dimension_numbers=('NWC', 'WIO', 'NWC'), feature_group_count=c)


def gated_delta_rule(q, k, v, beta, g):
    B, H, S, dk = q.shape
    dv = v.shape[-1]
    C = GDN_CHUNK
    n = S // C
    q, k, v = (t.reshape(B, H, n, C, t.shape[-1]) for t in (q, k, v))
    beta = beta.reshape(B, H, n, C)
    gc = jnp.cumsum(g.reshape(B, H, n, C), axis=-1)
    causal = jnp.tril(jnp.ones((C, C), bool))
    strict = jnp.tril(jnp.ones((C, C), bool), -1)
    diff = gc[..., :, None] - gc[..., None, :]
    decay = jnp.where(causal, jnp.exp(jnp.where(causal, diff, 0.0)), 0.0)
    kb = k * beta[..., None]
    m = jnp.where(strict, jnp.einsum('bhntd,bhnsd->bhnts', kb, k) * decay, 0.0)
    eye = jnp.eye(C, dtype=jnp.float32)
    t_inv = lax.linalg.triangular_solve(eye + m, jnp.broadcast_to(eye, m.shape),
                                        left_side=True, lower=True, unit_diagonal=True)
    u = t_inv @ (v * beta[..., None])
    w = t_inv @ (kb * jnp.exp(gc)[..., None])
    attn = jnp.where(causal, jnp.einsum('bhntd,bhnsd->bhnts', q, k) * decay, 0.0)

    def step(state, inp):
        q_i, k_i, u_i, w_i, a_i, gc_i = inp
        v_new = u_i - w_i @ state
        o = (q_i * jnp.exp(gc_i)[..., None]) @ state + a_i @ v_new
        g_last = gc_i[..., -1]
        state = state * jnp.exp(g_last)[..., None, None] + jnp.einsum(
            'bhcd,bhce->bhde', k_i * jnp.exp(g_last[..., None] - gc_i)[..., None], v_new)
        return state, o

    xs = tuple(jnp.moveaxis(t, 2, 0) for t in (q, k, u, w, attn, gc))
    _, o = lax.scan(step, jnp.zeros((B, H, dk, dv), jnp.float32), xs)
    return jnp.moveaxis(o, 0, 2).reshape(B, H, S, dv)


def gdn_mixer(qkv, beta_pre, a_pre, gate, conv_w, a_log, dt_bias, norm_g):
    dt = qkv.dtype
    B, S, _ = qkv.shape
    f32 = jnp.float32
    qkv = jax.nn.silu(causal_dwconv(qkv, conv_w)).astype(f32)
    q, k, v = jnp.split(qkv, [GDN_QK, 2 * GDN_QK], axis=-1)
    heads = lambda t, d: t.reshape(B, S, GDN_HEADS, d).transpose(0, 2, 1, 3)
    q = l2norm(heads(q, GDN_DK)) * GDN_DK ** -0.5
    k = l2norm(heads(k, GDN_DK))
    v = heads(v, GDN_DV)
    beta = jax.nn.sigmoid(beta_pre.astype(f32)).transpose(0, 2, 1)
    g = (-jnp.exp(a_log.astype(f32)) * jax.nn.softplus(a_pre.astype(f32) + dt_bias.astype(f32))).transpose(0, 2, 1)
    o = gated_delta_rule(q, k, v, beta, g).transpose(0, 2, 1, 3)
    o = rmsnorm(o, norm_g) * jax.nn.silu(gate.astype(f32).reshape(B, S, GDN_HEADS, GDN_DV))
    return o.reshape(B, S, GDN_V).astype(dt)


def nsa_mixer(q, kv, gate_pre, pe_k, pe_v, wk1, wk2, wv1, wv2):
    dt = q.dtype
    B, S, _ = q.shape
    G, R, dh = NSA_GROUPS, NSA_HEADS // NSA_GROUPS, NSA_DH
    f32 = jnp.float32
    q = q.astype(f32).reshape(B, S, G, R, dh).transpose(0, 2, 3, 1, 4) * dh ** -0.5
    kv = kv.astype(f32).reshape(B, S, 6, G, dh).transpose(2, 0, 3, 1, 4)
    k_cmp, v_cmp, k_slc, v_slc, k_win, v_win = (kv[i] for i in range(6))
    pos = jnp.arange(S)

    n_cmp = (S - CMP_LEN) // CMP_STRIDE + 1
    cmp_start = jnp.arange(n_cmp) * CMP_STRIDE
    blk = cmp_start[:, None] + jnp.arange(CMP_LEN)

    def compress(t, pe, w1, w2):
        tb = (t[:, :, blk] + pe).reshape(B, G, n_cmp, CMP_LEN * dh)
        return jax.nn.silu(tb @ w1) @ w2

    kc = compress(k_cmp, pe_k, wk1, wk2)
    vc = compress(v_cmp, pe_v, wv1, wv2)
    valid_c = (cmp_start[None, :] + CMP_LEN - 1) <= pos[:, None]
    s_c = jnp.einsum('bgrtd,bgnd->bgrtn', q, kc)
    p_c = jax.nn.softmax(jnp.where(valid_c, s_c, NEG_INF), axis=-1) * valid_c
    o_cmp = jnp.einsum('bgrtn,bgnd->bgrtd', p_c, vc)

    n_slc = S // SLC_LEN
    top_n = min(SLC_TOPN, n_slc)
    slc_start = jnp.arange(n_slc) * SLC_LEN
    overlap = ((cmp_start[:, None] < slc_start[None, :] + SLC_LEN)
               & (cmp_start[:, None] + CMP_LEN > slc_start[None, :])).astype(f32)
    imp = jnp.einsum('bgrtn,nj->bgtj', p_c, overlap)
    blk_id = jnp.arange(n_slc)[None, :]
    cur = (pos // SLC_LEN)[:, None]
    forced = (blk_id == 0) | (blk_id == cur) | (blk_id == cur - 1)
    imp = jnp.where(blk_id > cur, -FORCE, jnp.where(forced, FORCE, imp))
    _, sel = lax.top_k(imp, top_n)
    kb = k_slc.reshape(B, G, n_slc, SLC_LEN, dh)
    vb = v_slc.reshape(B, G, n_slc, SLC_LEN, dh)
    bi = jnp.arange(B)[:, None, None, None]
    gi = jnp.arange(G)[None, :, None, None]
    tq = SLC_Q_BLOCK
    nq = S // tq

    def slc_block(args):
        q_blk, sel_blk, t_blk = args
        kg = kb[bi, gi, sel_blk]
        vg = vb[bi, gi, sel_blk]
        s = jnp.einsum('bgrtd,bgtnld->bgrtnl', q_blk, kg)
        kpos = sel_blk[..., None] * SLC_LEN + jnp.arange(SLC_LEN)
        ok = (kpos <= t_blk[:, None, None])[:, :, None]
        s = jnp.where(ok, s, NEG_INF).reshape(B, G, R, tq, top_n * SLC_LEN)
        p = jax.nn.softmax(s, axis=-1).reshape(B, G, R, tq, top_n, SLC_LEN)
        return jnp.einsum('bgrtnl,bgtnld->bgrtd', p, vg)

    o_slc = lax.map(slc_block, (jnp.moveaxis(q.reshape(B, G, R, nq, tq, dh), 3, 0),
                                jnp.moveaxis(sel.reshape(B, G, nq, tq, top_n), 2, 0),
                                pos.reshape(nq, tq)))
    o_slc = jnp.moveaxis(o_slc, 0, 3).reshape(B, G, R, S, dh)

    tw = WIN_Q_BLOCK
    nw = S // tw
    pad = ((0, 0), (0, 0), (WIN_LEN, 0), (0, 0))
    kp = jnp.pad(k_win, pad)
    vp = jnp.pad(v_win, pad)

    def win_block(args):
        q_blk, i = args
        s0 = i * tw
        kw = lax.dynamic_slice_in_dim(kp, s0, WIN_LEN + tw, axis=2)
        vw = lax.dynamic_slice_in_dim(vp, s0, WIN_LEN + tw, axis=2)
        t = s0 + jnp.arange(tw)[:, None]
        p = s0 - WIN_LEN + jnp.arange(WIN_LEN + tw)[None, :]
        ok = (p <= t) & (p > t - WIN_LEN) & (p >= 0)
        s = jnp.einsum('bgrtd,bgkd->bgrtk', q_blk, kw)
        a = jax.nn.softmax(jnp.where(ok, s, NEG_INF), axis=-1)
        return jnp.einsum('bgrtk,bgkd->bgrtd', a, vw)

    o_win = lax.map(win_block, (jnp.moveaxis(q.reshape(B, G, R, nw, tw, dh), 3, 0), jnp.arange(nw)))
    o_win = jnp.moveaxis(o_win, 0, 3).reshape(B, G, R, S, dh)

    gates = jax.nn.sigmoid(gate_pre.astype(f32)).reshape(B, S, G, R, 3).transpose(4, 0, 2, 3, 1)[..., None]
    o = gates[0] * o_cmp + gates[1] * o_slc + gates[2] * o_win
    return o.transpose(0, 3, 1, 2, 4).reshape(B, S, NSA_Q).astype(dt)


def hgrn2_recurrence(q, k, v, logf):
    B, H, S, dk = q.shape
    dv = v.shape[-1]
    C = HG_CHUNK
    n = S // C
    to_chunks = lambda t: jnp.moveaxis(t.reshape(B, H, n, C, t.shape[-1]), 2, 0)
    b = jnp.cumsum(logf.reshape(B, H, n, C, dk), axis=3)
    xs = (to_chunks(q), to_chunks(k), to_chunks(v), jnp.moveaxis(b, 2, 0))
    causal = jnp.tril(jnp.ones((C, C), bool))[:, :, None]

    def step(state, inp):
        q_i, k_i, v_i, b_i = inp
        diff = b_i[:, :, :, None, :] - b_i[:, :, None, :, :]
        dec = jnp.where(causal, jnp.exp(jnp.where(causal, diff, 0.0)), 0.0)
        a = jnp.einsum('bhtd,bhsd,bhtsd->bhts', q_i, k_i, dec)
        o = jnp.einsum('bhtd,bhde->bhte', q_i * jnp.exp(b_i), state) + jnp.einsum('bhts,bhse->bhte', a, v_i)
        b_last = b_i[:, :, -1:, :]
        state = state * jnp.exp(b_last[:, :, 0, :, None]) + jnp.einsum(
            'bhsd,bhse->bhde', k_i * jnp.exp(b_last - b_i), v_i)
        return state, o

    _, o = lax.scan(step, jnp.zeros((B, H, dk, dv), jnp.float32), xs)
    return jnp.moveaxis(o, 0, 2).reshape(B, H, S, dv)


def hgrn2_mixer(q, f_pre, i, g, lb, norm_g):
    dt = q.dtype
    B, S, _ = q.shape
    f32 = jnp.float32
    heads = lambda t, d: t.astype(f32).reshape(B, S, HG_HEADS, d).transpose(0, 2, 1, 3)
    q = jax.nn.silu(heads(q, HG_DK))
    lb = lb.astype(f32).reshape(HG_HEADS, 1, HG_DK)
    f = lb + (1.0 - lb) * jax.nn.sigmoid(heads(f_pre, HG_DK))
    logf = jnp.log(jnp.maximum(f, LOG_FLOOR))
    k = 1.0 - f
    v = heads(i, HG_DV)
    o = hgrn2_recurrence(q, k, v, logf).transpose(0, 2, 1, 3).reshape(B, S, HG_V)
    o = rmsnorm(o, norm_g) * jax.nn.silu(g.astype(f32))
    return o.astype(dt)


def setup_inputs(seed: int = 0) -> dict:
    key = jax.random.key(seed)
    ks = iter(jax.random.split(key, 40))
    f32 = jnp.float32
    L = DEPTH

    def nrm(shape, scale):
        return jax.random.normal(next(ks), shape, f32) * scale

    def gain(shape):
        return 1.0 + nrm(shape, 0.02)

    x = nrm((BATCH, SEQ, D_MODEL), 1.0)
    ffn1_norm = gain((L, D_MODEL))
    ffn1_w_gate = nrm((L, D_MODEL, D_FF), D_MODEL ** -0.5)
    ffn1_w_up = nrm((L, D_MODEL, D_FF), D_MODEL ** -0.5)
    ffn1_w_down = nrm((L, D_FF, D_MODEL), D_FF ** -0.5)
    mix_norm = gain((L, D_MODEL))
    w_in = nrm((L, D_MODEL, N_IN), D_MODEL ** -0.5)
    gdn_conv = nrm((L, GDN_CONV, 2 * GDN_QK + GDN_V), GDN_CONV ** -0.5)
    gdn_a_log = jnp.log(jax.random.uniform(next(ks), (L, GDN_HEADS), f32, 1.0, 16.0))
    dt0 = jnp.exp(jax.random.uniform(next(ks), (L, GDN_HEADS), f32, math.log(1e-3), math.log(1e-1)))
    gdn_dt_bias = dt0 + jnp.log(-jnp.expm1(-dt0))
    gdn_out_norm = gain((L, GDN_DV))
    nsa_cmp_pe_k = nrm((L, CMP_LEN, NSA_DH), 0.1)
    nsa_cmp_pe_v = nrm((L, CMP_LEN, NSA_DH), 0.1)
    nsa_cmp_k_w1 = nrm((L, CMP_LEN * NSA_DH, NSA_DH), (CMP_LEN * NSA_DH) ** -0.5)
    nsa_cmp_k_w2 = nrm((L, NSA_DH, NSA_DH), NSA_DH ** -0.5)
    nsa_cmp_v_w1 = nrm((L, CMP_LEN * NSA_DH, NSA_DH), (CMP_LEN * NSA_DH) ** -0.5)
    nsa_cmp_v_w2 = nrm((L, NSA_DH, NSA_DH), NSA_DH ** -0.5)
    hgrn_lb_logits = nrm((L, HG_K), 1.0)
    hgrn_out_norm = gain((L, HG_V))
    w_proj_a = nrm((L, GDN_V, D_MODEL), GDN_V ** -0.5)
    w_proj_b = nrm((L, NSA_Q, D_MODEL), NSA_Q ** -0.5)
    w_proj_c = nrm((L, HG_V, D_MODEL), HG_V ** -0.5)
    w_out = nrm((L, D_MODEL, D_MODEL), D_MODEL ** -0.5)
    ffn2_norm = gain((L, D_MODEL))
    ffn2_w_gate = nrm((L, D_MODEL, D_FF), D_MODEL ** -0.5)
    ffn2_w_up = nrm((L, D_MODEL, D_FF), D_MODEL ** -0.5)
    ffn2_w_down = nrm((L, D_FF, D_MODEL), D_FF ** -0.5)
    final_norm = gain((D_MODEL,))
    return {'x': x, 'ffn1_norm': ffn1_norm, 'ffn1_w_gate': ffn1_w_gate, 'ffn1_w_up': ffn1_w_up,
            'ffn1_w_down': ffn1_w_down, 'mix_norm': mix_norm, 'w_in': w_in, 'gdn_conv': gdn_conv,
            'gdn_a_log': gdn_a_log, 'gdn_dt_bias': gdn_dt_bias, 'gdn_out_norm': gdn_out_norm,
            'nsa_cmp_pe_k': nsa_cmp_pe_k, 'nsa_cmp_pe_v': nsa_cmp_pe_v,
            'nsa_cmp_k_w1': nsa_cmp_k_w1, 'nsa_cmp_k_w2': nsa_cmp_k_w2,
            'nsa_cmp_v_w1': nsa_cmp_v_w1, 'nsa_cmp_v_w2': nsa_cmp_v_w2,
            'hgrn_lb_logits': hgrn_lb_logits, 'hgrn_out_norm': hgrn_out_norm,
            'w_proj_a': w_proj_a, 'w_proj_b': w_proj_b, 'w_proj_c': w_proj_c, 'w_out': w_out,
            'ffn2_norm': ffn2_norm, 'ffn2_w_gate': ffn2_w_gate, 'ffn2_w_up': ffn2_w_up,
            'ffn2_w_down': ffn2_w_down, 'final_norm': final_norm}


def reference(x, ffn1_norm, ffn1_w_gate, ffn1_w_up, ffn1_w_down, mix_norm, w_in, gdn_conv,
              gdn_a_log, gdn_dt_bias, gdn_out_norm, nsa_cmp_pe_k, nsa_cmp_pe_v,
              nsa_cmp_k_w1, nsa_cmp_k_w2, nsa_cmp_v_w1, nsa_cmp_v_w2,
              hgrn_lb_logits, hgrn_out_norm, w_proj_a, w_proj_b, w_proj_c, w_out,
              ffn2_norm, ffn2_w_gate, ffn2_w_up, ffn2_w_down, final_norm):
    B, S, _ = x.shape
    lb_p = jax.nn.softmax(hgrn_lb_logits.astype(jnp.float32), axis=0)
    lower_bounds = jnp.cumsum(lb_p, axis=0) - lb_p[0]
    for l in range(DEPTH):
        h = rmsnorm(x, ffn1_norm[l])
        x = x + 0.5 * swiglu(h, ffn1_w_gate[l], ffn1_w_up[l], ffn1_w_down[l])

        h = rmsnorm(x, mix_norm[l])
        u = h @ w_in[l]
        (gdn_qkv, gdn_beta, gdn_a, gdn_gate, nsa_q, nsa_kv, nsa_gate,
         hg_q, hg_f, hg_i, hg_g, merge_pre) = split_cols(u, IN_WIDTHS)
        y_a = gdn_mixer(gdn_qkv, gdn_beta, gdn_a, gdn_gate, gdn_conv[l], gdn_a_log[l],
                        gdn_dt_bias[l], gdn_out_norm[l])
        y_b = nsa_mixer(nsa_q, nsa_kv, nsa_gate, nsa_cmp_pe_k[l], nsa_cmp_pe_v[l],
                        nsa_cmp_k_w1[l], nsa_cmp_k_w2[l], nsa_cmp_v_w1[l], nsa_cmp_v_w2[l])
        y_c = hgrn2_mixer(hg_q, hg_f, hg_i, hg_g, lower_bounds[l], hgrn_out_norm[l])
        gates = jax.nn.sigmoid(merge_pre.astype(jnp.float32)).astype(x.dtype).reshape(B, S, N_BRANCH, D_MODEL)
        merged = (gates[:, :, 0] * (y_a @ w_proj_a[l])
                  + gates[:, :, 1] * (y_b @ w_proj_b[l])
                  + gates[:, :, 2] * (y_c @ w_proj_c[l]))
        x = x + merged @ w_out[l]

        h = rmsnorm(x, ffn2_norm[l])
        x = x + 0.5 * swiglu(h, ffn2_w_gate[l], ffn2_w_up[l], ffn2_w_down[l])
    return rmsnorm(x, final_norm)
```

```python
import contextlib
import numpy as np
import concourse.bass as bass
import concourse.mybir as mybir
from concourse.bass_utils import run_bass_kernel_spmd

F32 = mybir.dt.float32
F32R = mybir.dt.float32r


def R(ap):
    return ap.bitcast(F32R)
BF16 = mybir.dt.bfloat16
AF = mybir.ActivationFunctionType
ALU = mybir.AluOpType
AX = mybir.AxisListType

D = 2048
S = 2048
DFF = 5632
NC16 = 16
NF = 44
L = 2
EPS = 1e-6
TT = 512
NTT = S // TT

ENGS = ["sync", "scalar", "vector", "gpsimd", "tensor"]
EPOCH = 20000


class Prog:
    def __init__(self, nc):
        self.nc = nc
        self.ops = {e: [] for e in ENGS}
        self.cnt = {e: 0 for e in ENGS}
        self.epoch = {e: 0 for e in ENGS}
        self.sems = {}
        self.sem_ctx = []
        self.seen = {e: {} for e in ENGS}
        self.last_w = {}
        self.readers = {}
        self.dma_cnt = {}
        self.n_waits = 0

    def _sem(self, name):
        if name not in self.sems:
            ctx = self.nc.semaphore(name)
            h = ctx.__enter__()
            self.sem_ctx.append(ctx)
            self.sems[name] = h
        return self.sems[name]

    @staticmethod
    def _norm(reads, writes):
        r2, w2 = [], []
        for k in reads:
            if isinstance(k, tuple) and k[0] in ("ps", "psh"):
                w2.append(("ps", k[1]))
            else:
                r2.append(k)
        for k in writes:
            if isinstance(k, tuple) and k[0] in ("ps", "psh"):
                w2.append(("ps", k[1]))
            else:
                w2.append(k)
        return r2, w2

    def _deps(self, eng, reads, writes):
        evs = []
        for k in reads:
            ev = self.last_w.get(k)
            if ev is not None:
                evs.append(ev)
        for k in writes:
            rs = self.readers.get(k)
            if rs:
                evs.extend(rs)
            else:
                ev = self.last_w.get(k)
                if ev is not None:
                    evs.append(ev)
        waits = []
        seen = self.seen[eng]
        best = {}
        for (s, v) in evs:
            if seen.get(s, 0) < v and best.get(s, 0) < v:
                best[s] = v
        for s, v in best.items():
            seen[s] = v
            waits.append((s, v))
        return waits

    def _commit(self, ev, reads, writes):
        for k in writes:
            self.last_w[k] = ev
            self.readers[k] = []
        for k in reads:
            if k in writes:
                continue
            self.readers.setdefault(k, []).append(ev)

    def op(self, eng, fn, reads=(), writes=()):
        reads, writes = self._norm(reads, writes)
        waits = self._deps(eng, reads, writes)
        if self.cnt[eng] >= EPOCH:
            self.epoch[eng] += 1
            self.cnt[eng] = 0
        self.cnt[eng] += 1
        sname = f"e_{eng}_{self.epoch[eng]}"
        self._sem(sname)
        ev = (sname, self.cnt[eng])
        self.ops[eng].append((waits, fn, (sname, 1)))
        self.n_waits += len(waits)
        self._commit(ev, reads, writes)
        return ev

    def dma(self, eng, fn, chan, reads=(), writes=()):
        reads, writes = self._norm(reads, writes)
        waits = self._deps(eng, reads, writes)
        sname = "d_" + chan
        self._sem(sname)
        self.dma_cnt[sname] = self.dma_cnt.get(sname, 0) + 16
        ev = (sname, self.dma_cnt[sname])
        self.ops[eng].append((waits, fn, (sname, 16)))
        self.n_waits += len(waits)
        self._commit(ev, reads, writes)
        return ev

    def barrier(self):
        evs = []
        for e in ENGS:
            for ep in range(self.epoch[e] + 1):
                sname = f"e_{e}_{ep}"
                if sname in self.sems:
                    evs.append((sname, self.cnt[e] if ep == self.epoch[e] else EPOCH))
        for sname, v in self.dma_cnt.items():
            evs.append((sname, v))
        for e in ENGS:
            waits = []
            for (s, v) in evs:
                if self.seen[e].get(s, 0) < v:
                    self.seen[e][s] = v
                    waits.append((s, v))
            self.ops[e].append((waits, None, None))

    def emit(self):
        nc = self.nc
        P = self
        with nc.Block() as block:
            def mk(ename):
                def body(e):
                    for (waits, fn, inc) in P.ops[ename]:
                        for (s, v) in waits:
                            e.wait_ge(P.sems[s], v)
                        if fn is not None:
                            ins = fn(e)
                            ins.then_inc(P.sems[inc[0]], inc[1])
                return body
            block.sync(mk("sync"))
            block.scalar(mk("scalar"))
            block.vector(mk("vector"))
            block.gpsimd(mk("gpsimd"))
            block.tensor(mk("tensor"))
        for ctx in reversed(self.sem_ctx):
            ctx.__exit__(None, None, None)


class K:
    def __init__(self, nc, stop_after=None):
        self.nc = nc
        self.P = Prog(nc)
        self.stop_after = stop_after
        self.es = contextlib.ExitStack()
        self.psn = 0
        self.ps_rng = (0, 8)
        self.uid = 0
        self.dq = 0

    def dram_in(self, name, shape, dt=F32):
        return self.nc.dram_tensor(name, list(shape), dt, kind="ExternalInput").ap()

    def tile(self, stack, name, shape, dt):
        self.uid += 1
        return stack.enter_context(self.nc.sbuf_tensor(f"{name}_u{self.uid}", list(shape), dt))

    def ps(self):
        lo, hi = self.ps_rng
        i = lo + self.psn % (hi - lo)
        self.psn += 1
        return self.psum[i], ("ps", i)

    def psh(self, w, i):
        b = 2 * w + i % 2
        h = i // 2
        return self.psum[b][:, h * 256:(h + 1) * 256], ("ps", b)

    def interleave(self, jobs, W):
        pending = list(jobs)
        free = list(range(W))
        active = []
        while pending or active:
            while pending and free:
                w = free.pop(0)
                active.append((pending.pop(0)(w), w))
            nxt = []
            for gen, w in active:
                try:
                    next(gen)
                    nxt.append((gen, w))
                except StopIteration:
                    free.append(w)
            active = nxt

    def mm(self, out, pairs, reads, writes):
        n = len(pairs)

        def fn(e, out=out, pairs=pairs, n=n):
            ins = None
            for i, (a, b) in enumerate(pairs):
                ins = e.matmul(out, lhsT=a, rhs=b, start=(i == 0), stop=(i == n - 1))
            return ins
        self.P.op("tensor", fn, reads, writes)

    def act(self, out, in_, func, reads, writes, scale=None, bias=None, eng="scalar"):
        kw = {}
        if scale is not None:
            kw["scale"] = scale
        if bias is not None:
            kw["bias"] = bias
        self.P.op("scalar", lambda e, out=out, in_=in_, func=func, kw=kw: e.activation(out=out, in_=in_, func=func, **kw), reads, writes)

    def ts(self, out, in0, s1, op0, reads, writes, s2=None, op1=None, eng="vector"):
        if op1 is None:
            self.P.op(eng, lambda e: e.tensor_scalar(out=out, in0=in0, scalar1=s1, scalar2=None, op0=op0), reads, writes)
        else:
            self.P.op(eng, lambda e: e.tensor_scalar(out=out, in0=in0, scalar1=s1, scalar2=s2, op0=op0, op1=op1), reads, writes)

    def tt(self, out, in0, in1, op, reads, writes, eng="vector"):
        self.P.op(eng, lambda e: e.tensor_tensor(out=out, in0=in0, in1=in1, op=op), reads, writes)

    def stt(self, out, in0, scalar, in1, op0, op1, reads, writes):
        self.P.op("vector", lambda e: e.scalar_tensor_tensor(out=out, in0=in0, scalar=scalar, in1=in1, op0=op0, op1=op1), reads, writes)

    def copy(self, out, in_, reads, writes, eng="vector"):
        self.P.op(eng, lambda e: e.tensor_copy(out=out, in_=in_), reads, writes)

    def dma(self, out, in_, chan, reads, writes, eng=None):
        if eng is None:
            eng = ("sync", "scalar")[self.dq % 2]
            self.dq += 1
        self.P.dma(eng, lambda e: e.dma_start(out=out, in_=in_), chan, reads, writes)

    def rmsnorm_tile(self, xt, xkey, gcol, hT, hkey, sq, rstd, l_tag):
        pt, pk = self.ps()
        for c in range(NC16):
            sb = sq[c % 2]
            sk = ("sq", c % 2)
            self.act(sb[:, :], xt[:, c, :], AF.Square, [(xkey, c)], [sk])
            self.P.op("tensor", lambda e, c=c, sb=sb, pt=pt: e.matmul(pt[:, :], lhsT=self.ones_f[:, :], rhs=sb[:, :], start=(c == 0), stop=(c == NC16 - 1)),
                      [sk, "ones_f"], [pk])
        self.act(rstd[:, :], pt[:, :], AF.Sqrt, [pk, "epsb"], ["rstd"], scale=1.0 / D, bias=self.epsb[:, 0:1])
        self.P.op("vector", lambda e: e.reciprocal(out=rstd[:, :], in_=rstd[:, :]), ["rstd"], ["rstd"])
        for c in range(NC16):
            self.stt(hT[:, c, :], xt[:, c, :], gcol[:, c:c + 1], rstd[:, :], ALU.mult, ALU.mult, [(xkey, c), "rstd", "gains"], [hkey])

    def ffn(self, l, which, src_x, dst_x):
        nc, P = self.nc, self.P
        wg = self.w[f"ffn{which}_wg"][l]
        wu = self.w[f"ffn{which}_wu"][l]
        wd = self.w[f"ffn{which}_wd"][l]
        gcol = self.gains[:, (l * 3 + (0 if which == 1 else 2)) * 16:(l * 3 + (0 if which == 1 else 2)) * 16 + 16]
        with contextlib.ExitStack() as st:
            xt = self.tile(st, "f_xt", [128, NC16, TT], F32)
            hT = self.tile(st, "f_hT", [128, NC16, TT], BF16)
            aT = self.tile(st, "f_aT", [128, NF, TT], BF16)
            sq = [self.tile(st, f"f_sq{i}", [128, TT], F32) for i in range(2)]
            rstd = self.tile(st, "f_rstd", [128, TT], F32)
            sg = [self.tile(st, f"f_sg{i}", [128, TT], F32) for i in range(2)]
            wgb = [self.tile(st, f"f_wg{i}", [128, NC16, 128], BF16) for i in range(2)]
            wub = [self.tile(st, f"f_wu{i}", [128, NC16, 128], BF16) for i in range(2)]
            wdb = [self.tile(st, f"f_wd{i}", [128, NF, 128], BF16) for i in range(2)]
            for tt in range(NTT):
                tsl = slice(tt * TT, (tt + 1) * TT)
                xkeys = [("f_xt", c) for c in range(NC16)]
                self.dma(xt[:, :, :], src_x[:, :, tsl].rearrange("c p t -> p c t"), "f_xt", ["xs"], xkeys)
                self.rmsnorm_tile(xt, "f_xt", gcol, hT, "f_hT", sq, rstd, l)
                for f in range(NF):
                    b = f % 2
                    P.dma("gpsimd", lambda e, f=f, b=b: e.dma_start(out=wgb[b][:, :, :], in_=wg[f]), f"f_wg{b}", [], [("wg", b)])
                    P.dma("gpsimd", lambda e, f=f, b=b: e.dma_start(out=wub[b][:, :, :], in_=wu[f]), f"f_wu{b}", [], [("wu", b)])
                    pg, pgk = self.ps()
                    pu, puk = self.ps()
                    self.mm(pg[:, :], [(wgb[b][:, c, :], hT[:, c, :]) for c in range(NC16)], [("wg", b), "f_hT"], [pgk])
                    self.mm(pu[:, :], [(wub[b][:, c, :], hT[:, c, :]) for c in range(NC16)], [("wu", b), "f_hT"], [puk])
                    self.act(sg[b][:, :], pg[:, :], AF.Silu, [pgk], [("sg", b)])
                    self.tt(aT[:, f, :], sg[b][:, :], pu[:, :], ALU.mult, [("sg", b), puk], [("f_aT", f)])
                for dc in range(NC16):
                    b = dc % 2
                    P.dma("gpsimd", lambda e, dc=dc, b=b: e.dma_start(out=wdb[b][:, :, :], in_=wd[dc]), f"f_wd{b}", [], [("wd", b)])
                    py, pyk = self.ps()
                    self.mm(py[:, :], [(wdb[b][:, f, :], aT[:, f, :]) for f in range(NF)], [("wd", b)] + [("f_aT", f) for f in range(NF)], [pyk])
                    self.stt(xt[:, dc, :], py[:, :], 0.5, xt[:, dc, :], ALU.mult, ALU.add, [pyk, ("f_xt", dc)], [("f_xt", dc)])
                self.dma(dst_x[:, :, tsl].rearrange("c p t -> p c t"), xt[:, :, :], "f_xo", xkeys, ["xs"])
            P.barrier()

    def final_norm(self, src_x, out):
        with contextlib.ExitStack() as st:
            xt = self.tile(st, "n_xt", [128, NC16, TT], F32)
            ot = self.tile(st, "n_ot", [128, NC16, TT], F32)
            sq = [self.tile(st, f"n_sq{i}", [128, TT], F32) for i in range(2)]
            rstd = self.tile(st, "n_rstd", [128, TT], F32)
            gcol = self.gains[:, L * 3 * 16:L * 3 * 16 + 16]
            for tt in range(NTT):
                tsl = slice(tt * TT, (tt + 1) * TT)
                self.dma(xt[:, :, :], src_x[:, :, tsl].rearrange("c p t -> p c t"), "n_xt", ["xs"], [("n_xt", c) for c in range(NC16)])
                self.rmsnorm_tile(xt, "n_xt", gcol, ot, "n_ot", sq, rstd, 0)
                self.dma(out[:, :, tsl].rearrange("c p t -> p c t"), ot[:, :, :], "n_ot", ["n_ot"], ["out", "n_ot"])
            self.P.barrier()


    def wblk(self, l, blk):
        i = self.wbn % len(self.wb)
        self.wbn += 1
        t = self.wb[i]
        key = ("wb", i)
        src = self.w_in[l][blk]
        self.P.dma("gpsimd", lambda e, t=t, src=src: e.dma_start(out=t[:, :, :], in_=src), f"wb{i}", [], [key])
        return t, key

    def proj_fm(self, wt, wkey, t0, n):
        p, pk = self.ps()
        self.mm(p[:, 0:n], [(wt[:, c, :], self.hTm[:, c, t0:t0 + n]) for c in range(16)], [wkey, "hTm"], [pk])
        return p, pk

    def proj_tm(self, wt, wkey, t0, ncols):
        p, pk = self.ps()
        self.mm(p[:, 0:ncols], [(self.hTm[:, c, t0:t0 + 128], wt[:, c, 0:ncols]) for c in range(16)], [wkey, "hTm"], [pk])
        return p, pk

    def C(self, i):
        return self.cst[:, i * 128:(i + 1) * 128]

    def mix_prep(self, l, src_x, st):
        P = self.P
        gcol = self.gains[:, (l * 3 + 1) * 16:(l * 3 + 1) * 16 + 16]
        with contextlib.ExitStack() as s2:
            xt = self.tile(s2, "m_xt", [128, NC16, TT], F32)
            sq = [self.tile(s2, f"m_sq{i}", [128, TT], F32) for i in range(2)]
            rstd = self.tile(s2, "m_rstd", [128, TT], F32)
            wsm = self.tile(s2, "m_wsm", [128, NC16, 40], BF16)
            for tt in range(NTT):
                tsl = slice(tt * TT, (tt + 1) * TT)
                self.dma(xt[:, :, :], src_x[:, :, tsl].rearrange("c p t -> p c t"), "m_xt", ["xs"], [("m_xt", c) for c in range(NC16)])
                self.rmsnorm_tile(xt, "m_xt", gcol, self.hTm[:, :, tsl], "hTm", sq, rstd, l)
            P.dma("gpsimd", lambda e: e.dma_start(out=wsm[:, :, :], in_=self.w_sm[l]), "m_wsm", [], ["m_wsm"])
            sm = self.sm
            for ti in range(16):
                p, pk = self.proj_tm(wsm, "m_wsm", ti * 128, 40)
                self.copy(sm[:, ti, :], p[:, 0:40], [pk], ["sm"])
            self.act(sm[:, :, 0:8], sm[:, :, 0:8], AF.Sigmoid, ["sm"], ["sm"])
            self.act(sm[:, :, 16:40], sm[:, :, 16:40], AF.Sigmoid, ["sm"], ["sm"])
            gp = self.gpar
            xg = self.tile(s2, "m_xg", [128, 16, 8], F32)
            ax = self.tile(s2, "m_ax", [128, 16, 8], F32)
            nea = self.tile(s2, "m_nea", [128, 16, 8], F32)
            self.tt(xg[:, :, :], sm[:, :, 8:16], gp[:, l, 0, :, :], ALU.add, ["sm", "gpar"], ["m_xg"])
            self.ts(ax[:, :, :], xg[:, :, :], -1.0, ALU.mult, ["m_xg"], ["m_ax"])
            self.tt(ax[:, :, :], ax[:, :, :], xg[:, :, :], ALU.max, ["m_ax", "m_xg"], ["m_ax"])
            self.act(ax[:, :, :], ax[:, :, :], AF.Exp, ["m_ax"], ["m_ax"], scale=-1.0)
            self.act(ax[:, :, :], ax[:, :, :], AF.Ln, ["m_ax", "oneb"], ["m_ax"], bias=self.oneb[:, 0:1])
            self.stt(xg[:, :, :], xg[:, :, :], 0.0, ax[:, :, :], ALU.max, ALU.add, ["m_xg", "m_ax"], ["m_xg"])
            self.act(nea[:, :, :], gp[:, l, 1, :, :], AF.Exp, ["gpar"], ["m_nea"])
            self.stt(self.gg[:, :, :], xg[:, :, :], -1.0, nea[:, :, :], ALU.mult, ALU.mult, ["m_xg", "m_nea"], ["gg"])
            for ti in range(16):
                p, pk = self.ps()
                def fn(e, p=p, ti=ti, gg_=self.gg):
                    e.matmul(p[:, 0:8], lhsT=self.C(1), rhs=gg_[:, ti, :], start=True, stop=True)
                    return e.matmul(p[:, 8:16], lhsT=self.C(4), rhs=gg_[:, ti, :], start=True, stop=True)
                P.op("tensor", fn, ["gg", "cst"], [pk])
                self.copy(self.gcl[:, ti, :], p[:, 0:16], [pk], ["gcl"])
            self.act(self.egcl[:, :, :], self.gcl[:, :, :], AF.Exp, ["gcl"], ["egcl"])
            P.barrier()

    def hgrn(self, l, ys):
        P = self.P
        H = 1024
        with contextlib.ExitStack() as st:
            B = {n: self.tile(st, "h_" + n, [128, H], F32) for n in ("q", "k", "g", "t", "b", "qd", "kd", "qe", "ks", "o")}
            vtm = self.tile(st, "h_vtm", [128, 8, 128], F32)
            ATs = [self.tile(st, f"h_AT{i}", [128, 128], F32) for i in range(2)]
            kstms = [self.tile(st, f"h_kstm{i}", [128, 128], F32) for i in range(2)]
            state = self.tile(st, "h_state", [128, 128], F32)
            dec = self.tile(st, "h_dec", [128, 16], F32)
            zT = self.tile(st, "h_zT", [128, H], BF16)
            sqb = self.tile(st, "h_sqb", [128, 512], F32)
            rmask = self.tile(st, "h_rmask", [128, H], F32)
            P.op("vector", lambda e: e.memset(rmask[:, :], 1.0), [], ["h_rmask"])
            P.op("vector", lambda e: e.memset(rmask[:, :].rearrange("p (n c) -> p n c", c=64)[:, :, 0:1], 0.0), ["h_rmask"], ["h_rmask"])
            P.op("vector", lambda e, sq_=self.ssqC: e.memset(sq_[:, :], 0.0), [], ["ssqC"])
            v3 = lambda t: t[:, :].rearrange("p (n c) -> p n c", c=64)
            for h in range(8):
                P.op("vector", lambda e: e.memset(state[:, :], 0.0), ["h_state"], ["h_state"])
                lbc = self.lb[:, l * 8 + h:l * 8 + h + 1]
                omlc = self.oml[:, l * 8 + h:l * 8 + h + 1]
                for half in range(2):
                    T0 = half * H
                    if half == 0:
                        if h == 0:
                            hnext = [self.wblk(l, blk) for blk in (52, 60, 68, 76)]
                        (wq, wqk), (wf, wfk), (wi, wik), (wg_, wgk) = hnext
                    for tb in range(2):
                        cs = slice(tb * 512, (tb + 1) * 512)
                        p, pk = self.proj_fm(wq, wqk, T0 + tb * 512, 512)
                        self.act(B["q"][:, cs], p[:, :], AF.Silu, [pk], ["h_q"])
                        p, pk = self.proj_fm(wf, wfk, T0 + tb * 512, 512)
                        self.act(B["k"][:, cs], p[:, :], AF.Sigmoid, [pk], ["h_k"])
                        p, pk = self.proj_fm(wg_, wgk, T0 + tb * 512, 512)
                        self.act(B["g"][:, cs], p[:, :], AF.Silu, [pk], ["h_g"])
                    for j in range(8):
                        p, pk = self.proj_tm(wi, wik, T0 + j * 128, 128)
                        self.copy(vtm[:, j, :], p[:, 0:128], [pk], ["h_vtm"], eng="scalar" if False else "vector")
                    if half == 1 and h < 7:
                        hnext = [self.wblk(l, blk + h + 1) for blk in (52, 60, 68, 76)]
                    self.ts(B["k"][:, :], B["k"][:, :], omlc, ALU.mult, ["h_k", "lb"], ["h_k"], s2=lbc, op1=ALU.add)
                    self.ts(B["t"][:, :], B["k"][:, :], 1e-30, ALU.max, ["h_k"], ["h_t"])
                    self.act(B["t"][:, :], B["t"][:, :], AF.Ln, ["h_t"], ["h_t"])
                    self.ts(B["k"][:, :], B["k"][:, :], -1.0, ALU.mult, ["h_k"], ["h_k"], s2=1.0, op1=ALU.add)
                    P.op("vector", lambda e: e.tensor_tensor_scan(out=B["b"][:, :], data0=rmask[:, :], data1=B["t"][:, :], initial=0.0, op0=ALU.mult, op1=ALU.add),
                         ["h_t", "h_rmask"], ["h_b"])
                    b3 = v3(B["b"])
                    self.act(B["qe"][:, :], B["b"][:, :], AF.Exp, ["h_b"], ["h_qe"])
                    self.act(dec[:, :], b3[:, :, 63], AF.Exp, ["h_b"], ["h_dec"])
                    self.tt(v3(B["t"]), b3, b3[:, :, 32:33].to_broadcast([128, 16, 64]), ALU.subtract, ["h_b"], ["h_t"])
                    self.act(B["qd"][:, :], B["t"][:, :], AF.Exp, ["h_t"], ["h_qd"])
                    self.act(B["kd"][:, :], B["t"][:, :], AF.Exp, ["h_t"], ["h_kd"], scale=-1.0)
                    self.tt(B["qe"][:, :], B["qe"][:, :], B["q"][:, :], ALU.mult, ["h_qe", "h_q"], ["h_qe"])
                    self.tt(B["qd"][:, :], B["qd"][:, :], B["q"][:, :], ALU.mult, ["h_qd", "h_q"], ["h_qd"])
                    self.tt(B["kd"][:, :], B["kd"][:, :], B["k"][:, :], ALU.mult, ["h_kd", "h_k"], ["h_kd"])
                    self.tt(v3(B["t"]), b3[:, :, 63:64].to_broadcast([128, 16, 64]), b3, ALU.subtract, ["h_b"], ["h_t"])
                    self.act(B["ks"][:, :], B["t"][:, :], AF.Exp, ["h_t"], ["h_ks"])
                    self.tt(B["ks"][:, :], B["ks"][:, :], B["k"][:, :], ALU.mult, ["h_ks", "h_k"], ["h_ks"])
                    for j in range(8):
                        js = slice(j * 128, (j + 1) * 128)
                        AT = ATs[j % 2]
                        kstm = kstms[j % 2]
                        pa, pak = self.ps()
                        self.mm(pa[:, 0:128], [(B["kd"][:, js], B["qd"][:, js])], ["h_kd", "h_qd"], [pak])
                        self.tt(AT[:, :], pa[:, 0:128], self.C(1), ALU.mult, [pak, "cst"], [("h_AT", j % 2)])
                        pt, ptk = self.ps()
                        P.op("tensor", lambda e, pt=pt, js=js: e.transpose(pt[:, 0:128], B["ks"][:, js], self.C(0)), ["h_ks", "cst"], [ptk])
                        self.copy(kstm[:, :], pt[:, 0:128], [ptk], [("h_kstm", j % 2)])
                        psns = []
                        for cc in range(2):
                            rs = slice(cc * 64, (cc + 1) * 64)
                            psn, psk = self.ps()
                            P.op("tensor", lambda e, psn=psn, rs=rs, j=j, kstm=kstm: e.matmul(psn[:, 0:128], lhsT=kstm[rs, :], rhs=vtm[rs, j, :], start=True, stop=True),
                                 [("h_kstm", j % 2), "h_vtm"], [psk])
                            psns.append((psn, psk))
                        po, pok = self.ps()
                        for cc in range(2):
                            rs = slice(cc * 64, (cc + 1) * 64)
                            cols = slice(j * 128 + cc * 64, j * 128 + cc * 64 + 64)
                            def fnp(e, po=po, rs=rs, cols=cols, j=j, AT=AT):
                                e.matmul(po[:, rs], lhsT=vtm[:, j, :], rhs=AT[:, rs], start=True, stop=False)
                                return e.matmul(po[:, rs], lhsT=state[:, :], rhs=B["qe"][:, cols], start=False, stop=True)
                            P.op("tensor", fnp, ["h_vtm", ("h_AT", j % 2), "h_state", "h_qe", pok], [pok])
                            psn, psk = psns[cc]
                            self.stt(state[:, :], state[:, :], dec[:, j * 2 + cc:j * 2 + cc + 1], psn[:, 0:128], ALU.mult, ALU.add, ["h_state", "h_dec", psk], ["h_state"])
                        self.copy(B["o"][:, js], po[:, 0:128], [pok], ["h_o"], eng="vector")
                    for tb in range(2):
                        cs = slice(tb * 512, (tb + 1) * 512)
                        self.act(sqb[:, :], B["o"][:, cs], AF.Square, ["h_o"], ["h_sqb"])
                        p, pk = self.ps()
                        self.mm(p[:, :], [(self.ones_f[:, :], sqb[:, :])], ["h_sqb", "ones_f"], [pk])
                        gs = slice(T0 + tb * 512, T0 + (tb + 1) * 512)
                        self.tt(self.ssqC[:, gs], self.ssqC[:, gs], p[:, :], ALU.add, [pk, "ssqC"], ["ssqC"])
                    self.stt(zT[:, :], B["o"][:, :], self.hgn[:, l * 8 + h:l * 8 + h + 1], B["g"][:, :], ALU.mult, ALU.mult, ["h_o", "h_g", "hgn"], ["h_zT"])
                    self.dma(ys[2, h, :, T0:T0 + H], zT[:, :], "h_zT", ["h_zT"], ["ys"])
            self.act(self.ssqC[:, :], self.ssqC[:, :], AF.Sqrt, ["ssqC", "epsb"], ["ssqC"], scale=1.0 / 1024.0, bias=self.epsb[:, 0:1])
            P.op("vector", lambda e, sq_=self.ssqC: e.reciprocal(out=sq_[:, :], in_=sq_[:, :]), ["ssqC"], ["ssqC"])
            P.barrier()

    def gdn(self, l, ys):
        P = self.P
        H = 1024
        with contextlib.ExitStack() as st:
            B = {n: self.tile(st, "g_" + n, [128, H], F32) for n in ("q", "k", "v", "g", "o")}
            stage = [self.tile(st, f"g_st{i}", [128, 515], F32) for i in range(3)]
            sq = self.tile(st, "g_sq", [128, 512], F32)
            rs_ = self.tile(st, "g_rs", [128, 512], F32)
            yT = self.tile(st, "g_yT", [128, 512], BF16)
            names = ("grep", "brep", "dec", "decT", "egcB", "A0", "A0T", "X1", "XT1", "X2", "XT2", "attnT", "Tt", "R0v", "R0w", "kdec", "u", "wTA", "wTB", "qg", "vnew", "state")
            import os
            GW = int(os.environ.get('GW', '4'))
            GPE = os.environ.get('GPE', 'gpsimd')
            RN = ("A0", "A0T", "X1", "XT1", "X2", "XT2", "Tt")
            MS = [{n: self.tile(st, f"g{w}_" + n, [128, 128], F32R if n in RN else F32) for n in names if n != "state"} for w in range(GW)]
            V = lambda ap: ap.bitcast(F32)
            state = self.tile(st, "g_state", [128, 128], F32)
            M = {"state": state}
            for w in range(GW):
                P.op("vector", lambda e, w=w: e.memset(MS[w]["wTA"][:, :], 0.0), [], [f"g{w}_wTA"])
                P.op("vector", lambda e, w=w: e.memset(MS[w]["wTB"][:, :], 0.0), [], [f"g{w}_wTB"])
            for h in range(8):
                P.op("vector", lambda e: e.memset(M["state"][:, :], 0.0), ["g_state"], ["g_state"])
                for i in range(3):
                    P.op("vector", lambda e, i=i: e.memset(stage[i][:, 0:3], 0.0), [("g_st", i)], [("g_st", i)])
                for half in range(2):
                    T0 = half * H
                    if half == 0:
                        if h == 0:
                            gnext = [self.wblk(l, blk) for blk in (0, 8, 16, 24)]
                        wts = gnext[0:3]
                        wg_, wgk = gnext[3]
                    for tb in range(2):
                        cs = slice(tb * 512, (tb + 1) * 512)
                        for i, nm in enumerate(("q", "k", "v")):
                            cw = lambda tap, i=i: self.gcw[:, l, i * 8 + h, tap:tap + 1]
                            p, pk = self.proj_fm(wts[i][0], wts[i][1], T0 + tb * 512, 512)
                            sk = ("g_st", i)
                            self.copy(stage[i][:, 3:515], p[:, :], [pk], [sk])
                            dst = B[nm][:, cs]
                            dk_ = "g_" + nm
                            self.ts(dst, stage[i][:, 3:515], cw(3), ALU.mult, [sk, "gcw"], [dk_])
                            for tap in (2, 1, 0):
                                self.stt(dst, stage[i][:, tap:tap + 512], cw(tap), dst, ALU.mult, ALU.add, [sk, "gcw", dk_], [dk_])
                            self.copy(stage[i][:, 0:3], stage[i][:, 512:515], [sk], [sk])
                            self.act(dst, dst, AF.Silu, [dk_], [dk_])
                        nrm = []
                        for i, nm in enumerate(("q", "k")):
                            dst = B[nm][:, cs]
                            dk_ = "g_" + nm
                            sk = ("g_st", i)
                            sqv = stage[i][:, 3:515]
                            rsv, rsk = (rs_, "g_rs") if i == 0 else (sq, "g_sq")
                            self.act(sqv, dst, AF.Square, [dk_], [sk])
                            pn, pnk = self.ps()
                            self.mm(pn[:, :], [(self.ones_f[:, :], sqv)], [sk, "ones_f"], [pnk])
                            self.act(rsv[:, :], pn[:, :], AF.Sqrt, [pnk, "epsb"], [rsk], bias=self.epsb[:, 0:1])
                            nrm.append((dst, dk_, rsv, rsk, nm))
                        for dst, dk_, rsv, rsk, nm in nrm:
                            P.op("vector", lambda e, rsv=rsv: e.reciprocal(out=rsv[:, :], in_=rsv[:, :]), [rsk], [rsk])
                            self.stt(dst, dst, (128.0 ** -0.5) if nm == "q" else 1.0, rsv[:, :], ALU.mult, ALU.mult, [dk_, rsk], [dk_])
                        p, pk = self.proj_fm(wg_, wgk, T0 + tb * 512, 512)
                        self.act(B["g"][:, cs], p[:, :], AF.Silu, [pk], ["g_g"])
                    if half == 1 and h < 7:
                        gnext = [self.wblk(l, blk + h + 1) for blk in (0, 8, 16, 24)]
                    rec_done = [-1]

                    def tile_job(j, w, half=half, h=h):
                        M = MS[w]
                        T = lambda n: f"g{w}_" + n
                        ti = half * 8 + j
                        js = slice(j * 128, (j + 1) * 128)
                        bcol = self.sm[:, ti, h:h + 1]
                        gcol_ = self.gg[:, ti, h:h + 1]
                        gccol = self.gcl[:, ti, h:h + 1]
                        egccol = self.egcl[:, ti, h:h + 1]
                        eglcol = self.egcl[:, ti, 8 + h:9 + h]
                        self.ts(M["grep"][:, :], self.ones_f[:, :], gcol_, ALU.mult, ["ones_f", "gg"], [T("grep")])
                        self.ts(M["brep"][:, :], self.ones_f[:, :], bcol, ALU.mult, ["ones_f", "sm"], [T("brep")])
                        yield
                        pb, pbk = self.psh(w, 0)
                        def fn(e, pb=pb):
                            e.matmul(pb[:, 0:128], lhsT=M["grep"][:, :], rhs=self.C(1), start=True, stop=True)
                            return e.matmul(pb[:, 128:256], lhsT=M["brep"][:, :], rhs=self.C(0), start=True, stop=True)
                        P.op("tensor", fn, [T("grep"), T("brep"), "cst"], [pbk])
                        pk2, pk2k = self.psh(w, 1)
                        def fn2(e, pk2=pk2, js=js):
                            e.matmul(pk2[:, 0:128], lhsT=B["k"][:, js], rhs=B["k"][:, js], start=True, stop=True)
                            return e.matmul(pk2[:, 128:256], lhsT=B["k"][:, js], rhs=B["q"][:, js], start=True, stop=True)
                        P.op("tensor", fn2, ["g_k", "g_q"], [pk2k])
                        ptk_, ptkk = self.psh(w, 2)
                        def fn3(e, ptk_=ptk_, js=js):
                            e.transpose(ptk_[:, 0:128], B["k"][:, js], self.C(0))
                            return e.transpose(ptk_[:, 128:256], B["v"][:, js], self.C(0))
                        P.op("tensor", fn3, ["g_k", "g_v", "cst"], [ptkk])
                        yield
                        self.ts(M["dec"][:, :], pb[:, 0:128], gccol, ALU.subtract, [pbk, "gcl"], [T("dec")], s2=0.0, op1=ALU.max)
                        self.act(M["dec"][:, :], M["dec"][:, :], AF.Exp, [T("dec")], [T("dec")], scale=-1.0)
                        yield
                        self.ts(M["decT"][:, :], pb[:, 0:128], gccol, ALU.subtract, [pbk, "gcl"], [T("decT")], s2=0.0, op1=ALU.min)
                        self.act(M["decT"][:, :], M["decT"][:, :], AF.Exp, [T("decT")], [T("decT")])
                        self.act(M["egcB"][:, :], pb[:, 0:128], AF.Exp, [pbk], [T("egcB")])
                        yield
                        self.stt(M["A0T"][:, :], pk2[:, 0:128], bcol, M["dec"][:, :], ALU.mult, ALU.mult, [pk2k, "sm", T("dec")], [T("A0T")])
                        yield
                        self.tt(M["A0T"][:, :], V(M["A0T"][:, :]), self.C(4), ALU.mult, [T("A0T"), "cst"], [T("A0T")], eng=GPE)
                        self.tt(M["A0"][:, :], pk2[:, 0:128], M["decT"][:, :], ALU.mult, [pk2k, T("decT")], [T("A0")])
                        yield
                        self.tt(M["A0"][:, :], V(M["A0"][:, :]), pb[:, 128:256], ALU.mult, [T("A0"), pbk], [T("A0")])
                        yield
                        self.tt(M["A0"][:, :], V(M["A0"][:, :]), self.C(2), ALU.mult, [T("A0"), "cst"], [T("A0")], eng=GPE)
                        self.tt(M["attnT"][:, :], pk2[:, 128:256], M["decT"][:, :], ALU.mult, [pk2k, T("decT")], [T("attnT")])
                        yield
                        self.tt(M["attnT"][:, :], M["attnT"][:, :], self.C(1), ALU.mult, [T("attnT"), "cst"], [T("attnT")], eng=GPE)
                        self.ts(M["R0v"][:, :], ptk_[:, 128:256], bcol, ALU.mult, [ptkk, "sm"], [T("R0v")])
                        yield
                        self.ts(M["R0w"][:, :], ptk_[:, 0:128], bcol, ALU.mult, [ptkk, "sm", "egcl"], [T("R0w")], s2=egccol, op1=ALU.mult)
                        yield
                        self.ts(M["kdec"][:, :], ptk_[:, 0:128], eglcol, ALU.mult, [ptkk, "egcl"], [T("kdec")])
                        self.tt(M["Tt"][:, :], self.C(0), V(M["A0"][:, :]), ALU.subtract, ["cst", T("A0")], [T("Tt")], eng=GPE)
                        self.tt(M["qg"][:, :], B["q"][:, js], M["egcB"][:, :], ALU.mult, ["g_q", T("egcB")], [T("qg")], eng=GPE)
                        yield
                        X, XT, Xn, XTn = "A0", "A0T", "X1", "XT1"
                        for lev in range(5):
                            px, pxk = self.psh(w, lev % 2)
                            def fn4(e, px=px, X=X, XT=XT):
                                e.matmul(px[:, 0:128], lhsT=M[XT][:, :], rhs=M[X][:, :], start=True, stop=True)
                                return e.matmul(px[:, 128:256], lhsT=M[X][:, :], rhs=M[XT][:, :], start=True, stop=True)
                            P.op("tensor", fn4, [T(X), T(XT)], [pxk])
                            yield
                            self.copy(M[Xn][:, :], px[:, 0:128], [pxk], [T(Xn)])
                            self.act(M[XTn][:, :], px[:, 128:256], AF.Identity, [pxk], [T(XTn)])
                            yield
                            X, XT = Xn, XTn
                            Xn, XTn = ("X2", "XT2") if X == "X1" else ("X1", "XT1")
                            pa, pak = self.psh(w, 2 + lev % 2)
                            self.mm(pa[:, 0:128], [(M[XT][:, :], M["Tt"][:, :])], [T(XT), T("Tt")], [pak])
                            yield
                            self.tt(M["Tt"][:, :], V(M["Tt"][:, :]), pa[:, 0:128], ALU.add, [T("Tt"), pak], [T("Tt")])
                            yield
                        pu, puk = self.psh(w, 1)
                        def fn5(e, pu=pu):
                            e.matmul(pu[:, 0:128], lhsT=V(M["Tt"][:, :]), rhs=M["R0v"][:, :], start=True, stop=True)
                            return e.matmul(pu[:, 128:256], lhsT=M["R0w"][:, :], rhs=V(M["Tt"][:, :]), start=True, stop=True)
                        P.op("tensor", fn5, [T("Tt"), T("R0v"), T("R0w")], [puk])
                        yield
                        self.copy(M["u"][:, :], pu[:, 0:128], [puk], [T("u")])
                        self.act(M["wTA"][:, 0:64], pu[:, 128:192], AF.Identity, [puk], [T("wTA")])
                        self.act(M["wTB"][:, 64:128], pu[:, 192:256], AF.Identity, [puk], [T("wTB")])
                        yield
                        while rec_done[0] != j - 1:
                            yield
                        po, pok = self.psh(w, 0)
                        for cc in range(2):
                            rs = slice(cc * 64, (cc + 1) * 64)
                            wn = "wTA" if cc == 0 else "wTB"
                            pv, pvk = self.psh(w, 2)
                            self.mm(pv[:, 0:128], [(M[wn][:, :], state[:, :])], [T(wn), "g_state"], [pvk])
                            yield
                            self.tt(M["vnew"][rs, :], M["u"][rs, :], pv[rs, 0:128], ALU.subtract, [T("u"), pvk], [T("vnew")])
                            yield
                            def fn6(e, po=po, rs=rs):
                                e.matmul(po[:, rs], lhsT=state[:, :], rhs=M["qg"][:, rs], start=True, stop=False)
                                return e.matmul(po[:, rs], lhsT=M["vnew"][rs, :], rhs=M["attnT"][rs, rs], start=False, stop=True)
                            P.op("tensor", fn6, ["g_state", T("qg"), T("vnew"), T("attnT"), pok], [pok])
                            psn, psk = self.psh(w, 3)
                            self.mm(psn[:, 0:128], [(M["kdec"][rs, :], M["vnew"][rs, :])], [T("kdec"), T("vnew")], [psk])
                            yield
                            ecol = M["egcB"][:, cc * 64 + 63:cc * 64 + 64]
                            self.stt(state[:, :], state[:, :], ecol, psn[:, 0:128], ALU.mult, ALU.add, ["g_state", T("egcB"), psk], ["g_state"] if os.environ.get("GREC", "1") == "1" else [T("fake")])
                            yield
                        self.copy(B["o"][:, js], po[:, 0:128], [pok], ["g_o"])
                        rec_done[0] = j
                        yield

                    self.interleave([(lambda w, j=j: tile_job(j, w)) for j in range(8)], GW)
                    for tb in range(2):
                        cs = slice(tb * 512, (tb + 1) * 512)
                        self.act(sq[:, :], B["o"][:, cs], AF.Square, ["g_o"], ["g_sq"])
                        pn, pnk = self.ps()
                        self.mm(pn[:, :], [(self.ones_f[:, :], sq[:, :])], ["g_sq", "ones_f"], [pnk])
                        self.act(rs_[:, :], pn[:, :], AF.Sqrt, [pnk, "epsb"], ["g_rs"], scale=1.0 / 128.0, bias=self.epsb[:, 0:1])
                        P.op("vector", lambda e: e.reciprocal(out=rs_[:, :], in_=rs_[:, :]), ["g_rs"], ["g_rs"])
                        self.stt(sq[:, :], B["o"][:, cs], self.gdn_n[:, l:l + 1], rs_[:, :], ALU.mult, ALU.mult, ["g_o", "g_rs", "gdn_n"], ["g_sq"])
                        self.tt(yT[:, :], sq[:, :], B["g"][:, cs], ALU.mult, ["g_sq", "g_g"], ["g_yT"])
                        self.dma(ys[0, h, :, T0 + tb * 512:T0 + (tb + 1) * 512], yT[:, :], "g_yT", ["g_yT"], ["ys"])
            P.barrier()

    def nsa(self, l, ys):
        P = self.P
        SC = 128.0 ** -0.5
        with contextlib.ExitStack() as st:
            qT = self.tile(st, "n_qT", [128, 4, S], BF16)
            kslc = self.tile(st, "n_kslc", [128, S], BF16)
            kwin = self.tile(st, "n_kwin", [128, S], BF16)
            vslc = self.tile(st, "n_vslc", [128, 16, 129], BF16)
            vwin = self.tile(st, "n_vwin", [128, 16, 129], BF16)
            kcT = self.tile(st, "n_kcT", [128, 128], BF16)
            vcA = self.tile(st, "n_vcA", [128, 161], BF16)
            vmask = self.tile(st, "n_vmask", [128, S], BF16)
            emat = self.tile(st, "n_emat", [32, 16, 128], BF16)
            amk = self.tile(st, "n_amk", [128, 16, 32], F32)
            bmk = self.tile(st, "n_bmk", [128, 16, 32], F32)
            w2 = self.tile(st, "n_w2", [128, 2, 128], BF16)
            peT = self.tile(st, "n_peT", [128, 2, 32], F32)
            P.dma("gpsimd", lambda e: e.dma_start(out=vmask[:, :], in_=self.nsa_c["vmask"]), "n_c0", [], ["n_vmask"])
            P.dma("gpsimd", lambda e: e.dma_start(out=emat[:, :, :], in_=self.nsa_c["emat"]), "n_c1", [], ["n_emat"])
            P.dma("gpsimd", lambda e: e.dma_start(out=vcA[:, 129:161], in_=self.nsa_c["ovl"]), "n_c2", [], ["n_vcA"])
            self.dma(amk[:, :, :], self.nsa_c["amk"], "n_c3", [], ["n_amk"])
            self.dma(bmk[:, :, :], self.nsa_c["bmk"], "n_c4", [], ["n_bmk"])
            P.dma("gpsimd", lambda e: e.dma_start(out=w2[:, :, :], in_=self.nsa_w2[l]), "n_c5", [], ["n_w2"])
            self.dma(peT[:, :, :], self.nsa_pe[l], "n_c6", [], ["n_peT"])
            P.op("vector", lambda e: e.memset(vcA[:, 128:129], 1.0), [], ["n_vcA1"])
            P.op("vector", lambda e: e.memset(vslc[:, :, 128:129], 1.0), [], ["n_vslc"])
            P.op("vector", lambda e: e.memset(vwin[:, :, 128:129], 1.0), [], ["n_vwin"])
            for g in range(2):
                with contextlib.ExitStack() as s2:
                    tT = self.tile(s2, "n_tT", [128, S], F32)
                    tpe = self.tile(s2, "n_tpe", [128, 32, 127], BF16)
                    w1 = self.tile(s2, "n_w1", [128, 32, 128], BF16)
                    hid = self.tile(s2, "n_hid", [128, 128], BF16)
                    for r in range(4):
                        wq, wqk = self.wblk(l, 32 + g * 4 + r)
                        for tb in range(4):
                            p, pk = self.proj_fm(wq, wqk, tb * 512, 512)
                            self.ts(qT[:, r, tb * 512:(tb + 1) * 512], p[:, :], SC, ALU.mult, [pk], ["n_qT"])
                    for nm, dst, slot in (("n_kslc", kslc, 2), ("n_kwin", kwin, 4)):
                        wk, wkk = self.wblk(l, 40 + slot * 2 + g)
                        for tb in range(4):
                            p, pk = self.proj_fm(wk, wkk, tb * 512, 512)
                            self.copy(dst[:, tb * 512:(tb + 1) * 512], p[:, :], [pk], [nm])
                    for nm, dst, slot in (("n_vslc", vslc, 3), ("n_vwin", vwin, 5)):
                        wv, wvk = self.wblk(l, 40 + slot * 2 + g)
                        for ti in range(16):
                            p, pk = self.proj_tm(wv, wvk, ti * 128, 128)
                            self.copy(dst[:, ti, 0:128], p[:, 0:128], [pk], [nm])
                    for kv in range(2):
                        wt, wtk = self.wblk(l, 40 + kv * 2 + g)
                        P.dma("gpsimd", lambda e, kv=kv, w1=w1: e.dma_start(out=w1[:, :, :], in_=self.nsa_w1[l][kv]), "n_w1", [], ["n_w1"])
                        for tb in range(4):
                            p, pk = self.proj_fm(wt, wtk, tb * 512, 512)
                            self.copy(tT[:, tb * 512:(tb + 1) * 512], p[:, :], [pk], ["n_tT"])
                        t3 = tT[:, :].rearrange("p (n s) -> p n s", s=16)
                        for j in range(32):
                            src = t3[:, 0:127, j] if j < 16 else t3[:, 1:128, j - 16]
                            self.ts(tpe[:, j, :], src, peT[:, kv, j:j + 1], ALU.add, ["n_tT", "n_peT"], [("n_tpe", j)])
                        ph, phk = self.ps()
                        self.mm(ph[:, 0:127], [(w1[:, j, :], tpe[:, j, :]) for j in range(32)], ["n_w1"] + [("n_tpe", j) for j in range(32)], [phk])
                        self.act(hid[:, 0:127], ph[:, 0:127], AF.Silu, [phk], ["n_hid"])
                        pc, pck = self.ps()
                        if kv == 0:
                            self.mm(pc[:, 0:127], [(w2[:, 0, :], hid[:, 0:127])], ["n_w2", "n_hid"], [pck])
                            self.copy(kcT[:, 0:127], pc[:, 0:127], [pck], ["n_kcT"])
                        else:
                            self.mm(pc[0:127, 0:128], [(hid[:, 0:127], w2[:, 1, :])], ["n_w2", "n_hid"], [pck])
                            self.copy(vcA[0:127, 0:128], pc[0:127, 0:128], [pck], ["n_vcA"])
                    P.barrier()
                with contextlib.ExitStack() as s3:
                    ybT = self.tile(s3, "n_ybT", [128, 4, S], BF16)
                    pT = [self.tile(s3, f"n_pT{i}", [128, 512], BF16) for i in range(3)]
                    oacc = self.tile(s3, "n_oacc", [128, 4, 128], F32)
                    impa = self.tile(s3, "n_impa", [128, 32], F32)
                    impt = self.tile(s3, "n_impt", [128, 32], F32)
                    top8 = self.tile(s3, "n_top8", [128, 8], F32)
                    rden = self.tile(s3, "n_rden", [128, 4], F32)
                    selT4 = self.tile(s3, "n_selT4", [32, 4, 128], BF16)
                    pTn = [0]
                    self.ps_rng = (0, 4)

                    def attend(qi, br, ktiles):
                        qs = slice(qi * 128, (qi + 1) * 128)
                        ncol = 161 if br == 0 else 129
                        po = [(self.psum[4 + r], ("ps", 4 + r), 0) for r in range(4)]
                        nkt = len(ktiles)
                        def stage1(ki):
                            kT_ap, vA_ap, nk, mmask, post, rk = ktiles[ki]
                            pS, pSk = self.ps()
                            pairs = [(kT_ap, qT[:, :, qs])]
                            if mmask is not None:
                                pairs.append((mmask, selT4[:, :, :]))
                            self.mm(pS[0:nk, :], pairs, ["n_qT", "n_selT4", "n_emat"] + rk, [pSk])
                            pb_ = pT[pTn[0] % 3]
                            pbk_ = ("n_pT", pTn[0] % 3)
                            pTn[0] += 1
                            self.act(pb_[0:nk, :], pS[0:nk, :], AF.Exp, [pSk], [pbk_])
                            if post is not None:
                                v_ = pb_[0:nk, :].rearrange("p (r t) -> p r t", r=4)
                                P.op("gpsimd", lambda e, v_=v_, post=post, nk=nk: e.tensor_tensor(out=v_, in0=v_, in1=post.unsqueeze(1).to_broadcast([nk, 4, 128]), op=ALU.mult),
                                     [pbk_, "cst", "n_vmask"], [pbk_])
                            return pb_, pbk_

                        def stage2(ki, pb_, pbk_):
                            kT_ap, vA_ap, nk, mmask, post, rk = ktiles[ki]
                            for r in range(4):
                                pr, prk, c0 = po[r]
                                P.op("tensor", lambda e, pr=pr, c0=c0, r=r, pb_=pb_, vA_ap=vA_ap, nk=nk, ki=ki: e.matmul(
                                    pr[:, c0:c0 + ncol], lhsT=pb_[0:nk, r * 128:(r + 1) * 128], rhs=vA_ap, start=(ki == 0), stop=(ki == nkt - 1)),
                                    [pbk_, prk] + rk, [prk])

                        prev = stage1(0)
                        for ki in range(1, nkt):
                            cur_ = stage1(ki)
                            stage2(ki - 1, *prev)
                            prev = cur_
                        stage2(nkt - 1, *prev)
                        for r in range(4):
                            pr, prk, c0 = po[r]
                            gc_ = self.sm[:, qi, 16 + (g * 4 + r) * 3 + br:16 + (g * 4 + r) * 3 + br + 1]
                            self.ts(rden[:, r:r + 1], pr[:, c0 + 128:c0 + 129], 1e-30, ALU.max, [prk], ["n_rden"])
                            P.op("vector", lambda e, r=r, rden=rden: e.reciprocal(out=rden[:, r:r + 1], in_=rden[:, r:r + 1]), ["n_rden"], ["n_rden"])
                            if br == 0:
                                if r == 0:
                                    self.ts(impa[:, :], pr[:, c0 + 129:c0 + 161], rden[:, r:r + 1], ALU.mult, [prk, "n_rden"], ["n_impa"])
                                else:
                                    self.stt(impa[:, :], pr[:, c0 + 129:c0 + 161], rden[:, r:r + 1], impa[:, :], ALU.mult, ALU.add, [prk, "n_rden", "n_impa"], ["n_impa"])
                            self.tt(rden[:, r:r + 1], rden[:, r:r + 1], gc_, ALU.mult, ["n_rden", "sm"], ["n_rden"])
                            if br == 0:
                                self.ts(oacc[:, r, :], pr[:, c0:c0 + 128], rden[:, r:r + 1], ALU.mult, [prk, "n_rden"], [("n_oacc", r)])
                            else:
                                self.stt(oacc[:, r, :], pr[:, c0:c0 + 128], rden[:, r:r + 1], oacc[:, r, :], ALU.mult, ALU.add, [prk, "n_rden", ("n_oacc", r)], [("n_oacc", r)])

                    for qi in range(16):
                        qs = slice(qi * 128, (qi + 1) * 128)
                        attend(qi, 0, [(kcT[:, 0:127], vcA[0:127, 0:161], 127, None, vmask[0:127, qs], ["n_kcT", "n_vcA", "n_vcA1"])])
                        self.tt(impt[:, :], impa[:, :], amk[:, qi, :], ALU.mult, ["n_impa", "n_amk"], ["n_impt"])
                        self.tt(impt[:, :], impt[:, :], bmk[:, qi, :], ALU.add, ["n_impt", "n_bmk"], ["n_impt"])
                        P.op("vector", lambda e, top8=top8, impt=impt: e.max(out=top8[:, :], in_=impt[:, :]), ["n_impt"], ["n_top8"])
                        self.ts(impt[:, :], impt[:, :], top8[:, 7:8], ALU.is_ge, ["n_impt", "n_top8"], ["n_impt"])
                        self.ts(impt[:, :], impt[:, :], -1.0, ALU.add, ["n_impt"], ["n_impt"], s2=30000.0, op1=ALU.mult)
                        psl, pslk = self.ps()
                        P.op("tensor", lambda e, psl=psl, impt=impt: e.transpose(psl[0:32, 0:128], impt[:, :], self.C(0)), ["n_impt", "cst"], [pslk])
                        self.copy(selT4[:, :, :], psl[0:32, 0:128].unsqueeze(1).to_broadcast([32, 4, 128]), [pslk], ["n_selT4"])
                        kt = []
                        for kj in range(qi + 1):
                            ks_ = slice(kj * 128, (kj + 1) * 128)
                            kt.append((kslc[:, ks_], vslc[:, kj, :], 128, emat[:, kj, :], self.C(5) if kj == qi else None, ["n_kslc", "n_vslc"]))
                        attend(qi, 1, kt)
                        kt = []
                        for kj in range(max(0, qi - 4), qi + 1):
                            ks_ = slice(kj * 128, (kj + 1) * 128)
                            post = self.C(5) if kj == qi else (self.C(6) if kj == qi - 4 else None)
                            kt.append((kwin[:, ks_], vwin[:, kj, :], 128, None, post, ["n_kwin", "n_vwin"]))
                        attend(qi, 2, kt)
                        for r in range(4):
                            pt, ptk = self.ps()
                            P.op("tensor", lambda e, pt=pt, r=r, oacc=oacc: e.transpose(pt[:, 0:128], oacc[:, r, :], self.C(0)), [("n_oacc", r), "cst"], [ptk])
                            self.copy(ybT[:, r, qs], pt[:, 0:128], [ptk], ["n_ybT"])
                    for r in range(4):
                        self.dma(ys[1, g * 4 + r, :, :], ybT[:, r, :], "n_ybT", ["n_ybT"], ["ys"])
                    self.ps_rng = (0, 8)
                    P.barrier()
            P.barrier()

    def merge(self, l, src_x, dst_x, ys):
        P = self.P
        with contextlib.ExitStack() as st:
            yt = [self.tile(st, f"mg_y{i}", [128, 8, TT], BF16) for i in range(3)]
            mg = self.tile(st, "mg_mg", [128, NC16, TT], BF16)
            sg = [self.tile(st, f"mg_sg{i}", [128, TT], F32) for i in range(3)]
            t0_ = self.tile(st, "mg_t0", [128, TT], F32)
            t1_ = self.tile(st, "mg_t1", [128, TT], F32)
            wp = [[self.tile(st, f"mg_wp{i}_{b}", [128, 8, 128], BF16) for b in range(2)] for i in range(3)]
            xc = [self.tile(st, f"mg_xc{b}", [128, TT], F32) for b in range(2)]
            wb_saved = self.wb
            self.wb = list(self.wb) + [self.tile(st, f"mg_wbx{i}", [128, NC16, 128], BF16) for i in range(4)]
            for tt in range(NTT):
                tsl = slice(tt * TT, (tt + 1) * TT)
                for i in range(3):
                    self.dma(yt[i][:, :, :], ys[i, :, :, tsl].rearrange("h p t -> p h t"), f"mg_y{i}", ["ys"], [("mg_y", i)])
                for dc in range(NC16):
                    b = dc % 2
                    pp = []
                    for i in range(3):
                        P.dma("gpsimd", lambda e, i=i, b=b, dc=dc: e.dma_start(out=wp[i][b][:, :, :], in_=self.w_proj[l][i][dc]), f"mg_wp{i}_{b}", [], [("mg_wp", i, b)])
                        p, pk = self.ps()
                        self.mm(p[:, :], [(wp[i][b][:, jc, :], yt[i][:, jc, :]) for jc in range(8)], [("mg_wp", i, b), ("mg_y", i)], [pk])
                        pp.append((p, pk))
                    for i in range(3):
                        wt, wtk = self.wblk(l, 84 + i * 16 + dc)
                        p, pk = self.proj_fm(wt, wtk, tt * TT, TT)
                        self.act(sg[i][:, :], p[:, :], AF.Sigmoid, [pk], [("mg_sg", i)])
                    self.tt(t0_[:, :], sg[0][:, :], pp[0][0][:, :], ALU.mult, [("mg_sg", 0), pp[0][1]], ["mg_t0"])
                    self.tt(t1_[:, :], sg[1][:, :], pp[1][0][:, :], ALU.mult, [("mg_sg", 1), pp[1][1]], ["mg_t1"])
                    self.tt(t0_[:, :], t0_[:, :], t1_[:, :], ALU.add, ["mg_t0", "mg_t1"], ["mg_t0"], eng="gpsimd")
                    self.tt(t1_[:, :], pp[2][0][:, :], self.ssqC[:, tsl], ALU.mult, [pp[2][1], "ssqC"], ["mg_t1"])
                    self.tt(t1_[:, :], t1_[:, :], sg[2][:, :], ALU.mult, ["mg_t1", ("mg_sg", 2)], ["mg_t1"], eng="gpsimd")
                    self.tt(mg[:, dc, :], t0_[:, :], t1_[:, :], ALU.add, ["mg_t0", "mg_t1"], [("mg_mg", dc)])
                for dc in range(NC16):
                    b = dc % 2
                    self.dma(xc[b][:, :], src_x[dc, :, tsl], f"mg_xc{b}", ["xs"], [("mg_xc", b)])
                    i = self.wbn % len(self.wb)
                    self.wbn += 1
                    wt, wtk = self.wb[i], ("wb", i)
                    P.dma("gpsimd", lambda e, wt=wt, dc=dc: e.dma_start(out=wt[:, :, :], in_=self.w_out[l][dc]), f"wb{i}", [], [wtk])
                    p, pk = self.ps()
                    self.mm(p[:, :], [(wt[:, c, :], mg[:, c, :]) for c in range(NC16)], [wtk] + [("mg_mg", c) for c in range(NC16)], [pk])
                    self.tt(xc[b][:, :], xc[b][:, :], p[:, :], ALU.add, [("mg_xc", b), pk], [("mg_xc", b)])
                    self.dma(dst_x[dc, :, tsl], xc[b][:, :], f"mg_xo{b}", [("mg_xc", b)], ["xs"])
            P.barrier()
            self.wb = wb_saved

    def build(self, stages=None, dbg=None):
        nc = self.nc
        P = self.P
        self.wbn = 0
        self.xin = self.dram_in("xT", [NC16, 128, S])
        need_ffn = stages is None or any(sg[0] == "ffn" for sg in stages)
        need_mix = stages is None or any(sg[0] != "ffn" for sg in stages)
        self.w = {}
        if need_ffn:
            for which in (1, 2):
                self.w[f"ffn{which}_wg"] = [self.dram_in(f"f{which}g{l}", [NF, 128, NC16, 128]) for l in range(L)]
                self.w[f"ffn{which}_wu"] = [self.dram_in(f"f{which}u{l}", [NF, 128, NC16, 128]) for l in range(L)]
                self.w[f"ffn{which}_wd"] = [self.dram_in(f"f{which}d{l}", [NC16, 128, NF, 128]) for l in range(L)]
        if need_mix:
            self.w_in = [self.dram_in(f"win{l}", [132, 128, NC16, 128]) for l in range(L)]
            self.w_sm = [self.dram_in(f"wsm{l}", [128, NC16, 40]) for l in range(L)]
        gains_d = self.dram_in("gains", [128, (L * 3 + 1) * 16])
        cst_d = self.dram_in("cst", [128, 7 * 128])
        gpar_d = self.dram_in("gpar", [128, L, 2, 16, 8])
        hgz_d = self.dram_in("hgz", [128, L * 8])
        hgn_d = self.dram_in("hgn", [128, L * 8])
        gcw_d = self.dram_in("gcw", [128, L, 24, 4])
        if need_mix:
            self.nsa_c = {"vmask": self.dram_in("n_vmask", [128, S]), "emat": self.dram_in("n_emat", [32, 16, 128]),
                          "ovl": self.dram_in("n_ovl", [128, 32]), "amk": self.dram_in("n_amk", [128, 16, 32]),
                          "bmk": self.dram_in("n_bmk", [128, 16, 32])}
            self.nsa_w1 = [[self.dram_in(f"n_w1_{l}_{kv}", [128, 32, 128]) for kv in range(2)] for l in range(L)]
            self.nsa_w2 = [self.dram_in(f"n_w2_{l}", [128, 2, 128]) for l in range(L)]
            self.nsa_pe = [self.dram_in(f"n_pe_{l}", [128, 2, 32]) for l in range(L)]
            self.w_proj = [[self.dram_in(f"wp{l}_{i}", [16, 128, 8, 128]) for i in range(3)] for l in range(L)]
            self.w_out = [self.dram_in(f"wo{l}", [16, 128, 16, 128]) for l in range(L)]
        gdnn_d = self.dram_in("gdnn", [128, L])
        out = nc.dram_tensor("outT", [NC16, 128, S], F32, kind="ExternalOutput").ap()
        xs = nc.dram_tensor("xs", [NC16, 128, S], F32, kind="Internal").ap()
        ys = nc.dram_tensor("ys", [3, 8, 128, S], BF16, kind="Internal").ap()
        if dbg == "ys":
            ysd = nc.dram_tensor("ysd", [3, 8, 128, S], BF16, kind="ExternalOutput").ap()
        with contextlib.ExitStack() as st:
            self.psum = [st.enter_context(nc.psum_tensor(f"psb{i}", [128, 512], F32)) for i in range(8)]
            self.ones_f = self.tile(st, "ones_f", [128, 128], F32)
            self.epsb = self.tile(st, "epsb", [128, 1], F32)
            self.oneb = self.tile(st, "oneb", [128, 1], F32)
            self.gains = self.tile(st, "gains_sb", [128, (L * 3 + 1) * 16], F32)
            self.cst = self.tile(st, "cst_sb", [128, 7 * 128], F32)
            self.gpar = self.tile(st, "gpar_sb", [128, L, 2, 16, 8], F32)
            self.hgz = self.tile(st, "hgz_sb", [128, L * 8], F32)
            self.hgn = self.tile(st, "hgn_sb", [128, L * 8], F32)
            self.lb = self.tile(st, "lb_sb", [128, L * 8], F32)
            self.gcw = self.tile(st, "gcw_sb", [128, L, 24, 4], F32)
            self.gdn_n = self.tile(st, "gdnn_sb", [128, L], F32)
            self.dma(self.gcw[:, :, :, :], gcw_d[:, :, :, :], "gcw", [], ["gcw"])
            self.dma(self.gdn_n[:, :], gdnn_d[:, :], "gdn_n", [], ["gdn_n"])
            self.oml = self.tile(st, "oml_sb", [128, L * 8], F32)
            P.op("vector", lambda e: e.memset(self.ones_f[:, :], 1.0), [], ["ones_f"])
            P.op("vector", lambda e: e.memset(self.epsb[:, :], EPS), [], ["epsb"])
            P.op("vector", lambda e: e.memset(self.oneb[:, :], 1.0), [], ["oneb"])
            self.dma(self.gains[:, :], gains_d[:, :], "gains", [], ["gains"])
            self.dma(self.cst[:, :], cst_d[:, :], "cst", [], ["cst"])
            self.dma(self.gpar[:, :, :, :, :], gpar_d[:, :, :, :, :], "gpar", [], ["gpar"])
            self.dma(self.hgz[:, :], hgz_d[:, :], "hgz", [], ["hgz"])
            self.dma(self.hgn[:, :], hgn_d[:, :], "hgn", [], ["hgn"])
            self.lower_bounds(st)
            if stages is None:
                stages = []
                for l in range(L):
                    stages += [("ffn", l, 1), ("mix", l), ("ffn", l, 2)]
                stages.append(("final",))
            cur = self.xin
            for sg in stages:
                if sg[0] == "ffn":
                    self.ffn(sg[1], sg[2], cur, xs)
                    cur = xs
                elif sg[0] == "final":
                    self.final_norm(cur, out)
                else:
                    l = sg[1]
                    parts = sg[2] if len(sg) > 2 else ("hgrn", "gdn", "nsa", "merge")
                    with contextlib.ExitStack() as ms:
                        self.hTm = self.tile(ms, "hTm", [128, NC16, S], BF16)
                        self.sm = self.tile(ms, "sm", [128, 16, 40], F32)
                        self.gg = self.tile(ms, "gg", [128, 16, 8], F32)
                        self.gcl = self.tile(ms, "gcl", [128, 16, 16], F32)
                        self.egcl = self.tile(ms, "egcl", [128, 16, 16], F32)
                        self.ssqC = self.tile(ms, "ssqC", [128, S], F32)
                        self.wb = [self.tile(ms, f"wb{i}", [128, NC16, 128], BF16) for i in range(4)]
                        self.mix_prep(l, cur, ms)
                        if "hgrn" in parts:
                            self.hgrn(l, ys)
                        if "gdn" in parts:
                            self.gdn(l, ys)
                        if "nsa" in parts:
                            self.nsa(l, ys)
                        if "merge" in parts:
                            self.merge(l, cur, xs, ys)
                            cur = xs
                        P.barrier()
            if dbg == "ys":
                self.dma(ysd[:, :, :, :], ys[:, :, :, :], "dbg", ["ys"], ["ysd"])
                P.barrier()
            elif dbg == "xs":
                self.dma(out[:, :, :], cur[:, :, :], "dbg", ["xs"], ["out"])
                P.barrier()
            P.emit()
        return nc

    def lower_bounds(self, st):
        P = self.P
        m = self.tile(st, "lb_m", [128, 8], F32)
        e = self.tile(st, "lb_e", [128, L * 8], F32)
        ssum = self.tile(st, "lb_s", [128, 8], F32)
        self.copy(m[:, :], self.hgz[:, 0:8], ["hgz"], ["lb_m"])
        for l in range(1, L):
            self.tt(m[:, :], m[:, :], self.hgz[:, l * 8:(l + 1) * 8], ALU.max, ["lb_m", "hgz"], ["lb_m"])
        for l in range(L):
            self.tt(e[:, l * 8:(l + 1) * 8], self.hgz[:, l * 8:(l + 1) * 8], m[:, :], ALU.subtract, ["hgz", "lb_m"], ["lb_e"])
        self.act(e[:, :], e[:, :], AF.Exp, ["lb_e"], ["lb_e"])
        self.copy(ssum[:, :], e[:, 0:8], ["lb_e"], ["lb_s"])
        for l in range(1, L):
            self.tt(ssum[:, :], ssum[:, :], e[:, l * 8:(l + 1) * 8], ALU.add, ["lb_s", "lb_e"], ["lb_s"])
        P.op("vector", lambda en: en.reciprocal(out=ssum[:, :], in_=ssum[:, :]), ["lb_s"], ["lb_s"])
        for l in range(L):
            self.tt(e[:, l * 8:(l + 1) * 8], e[:, l * 8:(l + 1) * 8], ssum[:, :], ALU.mult, ["lb_e", "lb_s"], ["lb_e"])
        self.copy(self.lb[:, 0:8], e[:, 0:8], ["lb_e"], ["lb"])
        for l in range(1, L):
            self.tt(self.lb[:, l * 8:(l + 1) * 8], self.lb[:, (l - 1) * 8:l * 8], e[:, l * 8:(l + 1) * 8], ALU.add, ["lb", "lb_e"], ["lb"])
        for l in range(L):
            self.tt(self.lb[:, l * 8:(l + 1) * 8], self.lb[:, l * 8:(l + 1) * 8], e[:, 0:8], ALU.subtract, ["lb", "lb_e"], ["lb"])
        self.ts(self.oml[:, :], self.lb[:, :], -1.0, ALU.mult, ["lb"], ["lb"] if False else ["oml"], s2=1.0, op1=ALU.add)


def _colblock(w):
    kd, n = w.shape
    return np.ascontiguousarray(w.reshape(kd // 128, 128, n // 128, 128).transpose(2, 1, 0, 3))


def _consts():
    p = np.arange(128)[:, None]
    f = np.arange(128)[None, :]
    same = (p // 64) == (f // 64)
    mats = [p == f, (p <= f) & same, (p < f) & same, (p >= f) & same, (p > f) & same, p <= f, p > f]
    return np.ascontiguousarray(np.concatenate([m.astype(np.float32) for m in mats], axis=1))


def prep_weights(inp, need_ffn=True, need_mix=True):
    m = {}
    if need_ffn:
        for which in (1, 2):
            for l in range(L):
                m[f"f{which}g{l}"] = _colblock(inp[f"ffn{which}_w_gate"][l])
                m[f"f{which}u{l}"] = _colblock(inp[f"ffn{which}_w_up"][l])
                m[f"f{which}d{l}"] = _colblock(inp[f"ffn{which}_w_down"][l])
    if need_mix:
        for l in range(L):
            w = inp["w_in"][l]
            big = np.concatenate([w[:, 0:3072], w[:, 3088:6672], w[:, 6696:16936]], axis=1)
            m[f"win{l}"] = _colblock(big)
            small = np.concatenate([w[:, 3072:3088], w[:, 6672:6696]], axis=1)
            m[f"wsm{l}"] = np.ascontiguousarray(small.reshape(16, 128, 40).transpose(1, 0, 2))
    gl = []
    for l in range(L):
        for nm in ("ffn1_norm", "mix_norm", "ffn2_norm"):
            gl.append(inp[nm][l].reshape(16, 128).T)
    gl.append(inp["final_norm"].reshape(16, 128).T)
    m["gains"] = np.ascontiguousarray(np.concatenate(gl, axis=1)).astype(np.float32)
    m["cst"] = _consts()
    gpar = np.zeros((128, L, 2, 16, 8), np.float32)
    for l in range(L):
        gpar[:, l, 0] = inp["gdn_dt_bias"][l][None, None, :]
        gpar[:, l, 1] = inp["gdn_a_log"][l][None, None, :]
    m["gpar"] = gpar
    if need_mix:
        t = np.arange(S)
        n = np.arange(128)
        vm = ((16 * n[:, None] + 31) <= t[None, :]) & (n[:, None] < 127)
        m["n_vmask"] = vm.astype(np.float32)
        j = np.arange(32)[:, None, None]
        kj = np.arange(16)[None, :, None]
        ss = np.arange(128)[None, None, :]
        m["n_emat"] = (j == 2 * kj + ss // 64).astype(np.float32)
        cs = np.arange(128) * 16
        sl = np.arange(32) * 64
        ov = ((cs[:, None] < sl[None, :] + 64) & (cs[:, None] + 32 > sl[None, :]) & (np.arange(128)[:, None] < 127))
        m["n_ovl"] = ov.astype(np.float32)
        cur = (t // 64)[:, None]
        bid = np.arange(32)[None, :]
        fut = bid > cur
        forced = (bid == 0) | (bid == cur) | (bid == cur - 1)
        A = np.where(fut | forced, 0.0, 1.0).astype(np.float32)
        Bm = np.where(fut, -1e6, np.where(forced, 1e6, 0.0)).astype(np.float32)
        m["n_amk"] = np.ascontiguousarray(A.reshape(16, 128, 32).transpose(1, 0, 2))
        m["n_bmk"] = np.ascontiguousarray(Bm.reshape(16, 128, 32).transpose(1, 0, 2))
        for l in range(L):
            for kv, nm in enumerate(("k", "v")):
                m[f"n_w1_{l}_{kv}"] = np.ascontiguousarray(inp[f"nsa_cmp_{nm}_w1"][l].reshape(32, 128, 128).transpose(1, 0, 2))
            m[f"n_w2_{l}"] = np.ascontiguousarray(np.stack([inp["nsa_cmp_k_w2"][l], inp["nsa_cmp_v_w2"][l]], axis=1))
            m[f"n_pe_{l}"] = np.ascontiguousarray(np.stack([inp["nsa_cmp_pe_k"][l].T, inp["nsa_cmp_pe_v"][l].T], axis=1))
            for i, nm in enumerate(("a", "b", "c")):
                m[f"wp{l}_{i}"] = _colblock(inp[f"w_proj_{nm}"][l])
            m[f"wo{l}"] = _colblock(inp["w_out"][l])
    m["gcw"] = np.ascontiguousarray(inp["gdn_conv"].reshape(L, 4, 24, 128).transpose(3, 0, 2, 1))
    m["gdnn"] = np.ascontiguousarray(inp["gdn_out_norm"].T)
    m["hgz"] = np.ascontiguousarray(inp["hgrn_lb_logits"].reshape(L, 8, 128).transpose(2, 0, 1).reshape(128, L * 8))
    m["hgn"] = np.ascontiguousarray(inp["hgrn_out_norm"].reshape(L, 8, 128).transpose(2, 0, 1).reshape(128, L * 8))
    return m


_CACHE = {}


def run(inp, stages=None, dbg=None, n_cores=8, xT_override=None):
    nc = bass.Bass("TRN2", target_bir_lowering=False)
    kb = K(nc)
    kb.build(stages, dbg)
    need_ffn = stages is None or any(sg[0] == "ffn" for sg in stages)
    need_mix = stages is None or any(sg[0] != "ffn" for sg in stages)
    wm = prep_weights(inp, need_ffn, need_mix)
    x = inp["x"]
    in_maps = []
    for b in range(n_cores):
        d = dict(wm)
        if xT_override is not None:
            d["xT"] = np.ascontiguousarray(xT_override.reshape(NC16, 128, S))
        else:
            d["xT"] = np.ascontiguousarray(x[b].T.reshape(NC16, 128, S))
        in_maps.append(d)
    res = run_bass_kernel_spmd(nc, in_maps, core_ids=list(range(n_cores)))
    return res


def kernel(**inputs):
    inp = {k: np.asarray(v) for k, v in inputs.items()}
    res = run(inp)
    outs = [np.asarray(r["outT"]).reshape(D, S).T for r in res.results]
    return np.ascontiguousarray(np.stack(outs, axis=0)).astype(np.float32)
```

```python
import contextlib
import numpy as np
import concourse.bass as bass
import concourse.mybir as mybir
from concourse.bass_utils import run_bass_kernel_spmd

F32 = mybir.dt.float32
F32R = mybir.dt.float32r


def R(ap):
    return ap.bitcast(F32R)
BF16 = mybir.dt.bfloat16
AF = mybir.ActivationFunctionType
ALU = mybir.AluOpType
AX = mybir.AxisListType

D = 2048
S = 2048
DFF = 5632
NC16 = 16
NF = 44
L = 2
EPS = 1e-6
TT = 512
NTT = S // TT

ENGS = ["sync", "scalar", "vector", "gpsimd", "tensor"]
EPOCH = 20000


class Prog:
    def __init__(self, nc):
        self.nc = nc
        self.ops = {e: [] for e in ENGS}
        self.cnt = {e: 0 for e in ENGS}
        self.epoch = {e: 0 for e in ENGS}
        self.sems = {}
        self.sem_ctx = []
        self.seen = {e: {} for e in ENGS}
        self.last_w = {}
        self.readers = {}
        self.dma_cnt = {}
        self.n_waits = 0

    def _sem(self, name):
        if name not in self.sems:
            ctx = self.nc.semaphore(name)
            h = ctx.__enter__()
            self.sem_ctx.append(ctx)
            self.sems[name] = h
        return self.sems[name]

    @staticmethod
    def _norm(reads, writes):
        r2, w2 = [], []
        for k in reads:
            if isinstance(k, tuple) and k[0] in ("ps", "psh"):
                w2.append(("ps", k[1]))
            else:
                r2.append(k)
        for k in writes:
            if isinstance(k, tuple) and k[0] in ("ps", "psh"):
                w2.append(("ps", k[1]))
            else:
                w2.append(k)
        return r2, w2

    def _deps(self, eng, reads, writes):
        evs = []
        for k in reads:
            ev = self.last_w.get(k)
            if ev is not None:
                evs.append(ev)
        for k in writes:
            rs = self.readers.get(k)
            if rs:
                evs.extend(rs)
            else:
                ev = self.last_w.get(k)
                if ev is not None:
                    evs.append(ev)
        waits = []
        seen = self.seen[eng]
        best = {}
        for (s, v) in evs:
            if seen.get(s, 0) < v and best.get(s, 0) < v:
                best[s] = v
        for s, v in best.items():
            seen[s] = v
            waits.append((s, v))
        return waits

    def _commit(self, ev, reads, writes):
        for k in writes:
            self.last_w[k] = ev
            self.readers[k] = []
        for k in reads:
            if k in writes:
                continue
            self.readers.setdefault(k, []).append(ev)

    def op(self, eng, fn, reads=(), writes=()):
        reads, writes = self._norm(reads, writes)
        waits = self._deps(eng, reads, writes)
        if self.cnt[eng] >= EPOCH:
            self.epoch[eng] += 1
            self.cnt[eng] = 0
        self.cnt[eng] += 1
        sname = f"e_{eng}_{self.epoch[eng]}"
        self._sem(sname)
        ev = (sname, self.cnt[eng])
        self.ops[eng].append((waits, fn, (sname, 1)))
        self.n_waits += len(waits)
        self._commit(ev, reads, writes)
        return ev

    def dma(self, eng, fn, chan, reads=(), writes=()):
        reads, writes = self._norm(reads, writes)
        waits = self._deps(eng, reads, writes)
        sname = "d_" + chan
        self._sem(sname)
        self.dma_cnt[sname] = self.dma_cnt.get(sname, 0) + 16
        ev = (sname, self.dma_cnt[sname])
        self.ops[eng].append((waits, fn, (sname, 16)))
        self.n_waits += len(waits)
        self._commit(ev, reads, writes)
        return ev

    def barrier(self):
        evs = []
        for e in ENGS:
            for ep in range(self.epoch[e] + 1):
                sname = f"e_{e}_{ep}"
                if sname in self.sems:
                    evs.append((sname, self.cnt[e] if ep == self.epoch[e] else EPOCH))
        for sname, v in self.dma_cnt.items():
            evs.append((sname, v))
        for e in ENGS:
            waits = []
            for (s, v) in evs:
                if self.seen[e].get(s, 0) < v:
                    self.seen[e][s] = v
                    waits.append((s, v))
            self.ops[e].append((waits, None, None))

    def emit(self):
        nc = self.nc
        P = self
        with nc.Block() as block:
            def mk(ename):
                def body(e):
                    for (waits, fn, inc) in P.ops[ename]:
                        for (s, v) in waits:
                            e.wait_ge(P.sems[s], v)
                        if fn is not None:
                            ins = fn(e)
                            ins.then_inc(P.sems[inc[0]], inc[1])
                return body
            block.sync(mk("sync"))
            block.scalar(mk("scalar"))
            block.vector(mk("vector"))
            block.gpsimd(mk("gpsimd"))
            block.tensor(mk("tensor"))
        for ctx in reversed(self.sem_ctx):
            ctx.__exit__(None, None, None)


class K:
    def __init__(self, nc, stop_after=None):
        self.nc = nc
        self.P = Prog(nc)
        self.stop_after = stop_after
        self.es = contextlib.ExitStack()
        self.psn = 0
        self.ps_rng = (0, 8)
        self.uid = 0
        self.dq = 0

    def dram_in(self, name, shape, dt=F32):
        return self.nc.dram_tensor(name, list(shape), dt, kind="ExternalInput").ap()

    def tile(self, stack, name, shape, dt):
        self.uid += 1
        return stack.enter_context(self.nc.sbuf_tensor(f"{name}_u{self.uid}", list(shape), dt))

    def ps(self):
        lo, hi = self.ps_rng
        i = lo + self.psn % (hi - lo)
        self.psn += 1
        return self.psum[i], ("ps", i)

    def psh(self, w, i):
        b = 2 * w + i % 2
        h = i // 2
        return self.psum[b][:, h * 256:(h + 1) * 256], ("ps", b)

    def interleave(self, jobs, W):
        pending = list(jobs)
        free = list(range(W))
        active = []
        while pending or active:
            while pending and free:
                w = free.pop(0)
                active.append((pending.pop(0)(w), w))
            nxt = []
            for gen, w in active:
                try:
                    next(gen)
                    nxt.append((gen, w))
                except StopIteration:
                    free.append(w)
            active = nxt

    def mm(self, out, pairs, reads, writes):
        n = len(pairs)

        def fn(e, out=out, pairs=pairs, n=n):
            ins = None
            for i, (a, b) in enumerate(pairs):
                ins = e.matmul(out, lhsT=a, rhs=b, start=(i == 0), stop=(i == n - 1))
            return ins
        self.P.op("tensor", fn, reads, writes)

    def act(self, out, in_, func, reads, writes, scale=None, bias=None, eng="scalar"):
        kw = {}
        if scale is not None:
            kw["scale"] = scale
        if bias is not None:
            kw["bias"] = bias
        self.P.op("scalar", lambda e, out=out, in_=in_, func=func, kw=kw: e.activation(out=out, in_=in_, func=func, **kw), reads, writes)

    def ts(self, out, in0, s1, op0, reads, writes, s2=None, op1=None, eng="vector"):
        if op1 is None:
            self.P.op(eng, lambda e: e.tensor_scalar(out=out, in0=in0, scalar1=s1, scalar2=None, op0=op0), reads, writes)
        else:
            self.P.op(eng, lambda e: e.tensor_scalar(out=out, in0=in0, scalar1=s1, scalar2=s2, op0=op0, op1=op1), reads, writes)

    def tt(self, out, in0, in1, op, reads, writes, eng="vector"):
        self.P.op(eng, lambda e: e.tensor_tensor(out=out, in0=in0, in1=in1, op=op), reads, writes)

    def stt(self, out, in0, scalar, in1, op0, op1, reads, writes):
        self.P.op("vector", lambda e: e.scalar_tensor_tensor(out=out, in0=in0, scalar=scalar, in1=in1, op0=op0, op1=op1), reads, writes)

    def copy(self, out, in_, reads, writes, eng="vector"):
        self.P.op(eng, lambda e: e.tensor_copy(out=out, in_=in_), reads, writes)

    def dma(self, out, in_, chan, reads, writes, eng=None):
        if eng is None:
            eng = ("sync", "scalar")[self.dq % 2]
            self.dq += 1
        self.P.dma(eng, lambda e: e.dma_start(out=out, in_=in_), chan, reads, writes)

    def rmsnorm_tile(self, xt, xkey, gcol, hT, hkey, sq, rstd, l_tag):
        pt, pk = self.ps()
        for c in range(NC16):
            sb = sq[c % 2]
            sk = ("sq", c % 2)
            self.act(sb[:, :], xt[:, c, :], AF.Square, [(xkey, c)], [sk])
            self.P.op("tensor", lambda e, c=c, sb=sb, pt=pt: e.matmul(pt[:, :], lhsT=self.ones_f[:, :], rhs=sb[:, :], start=(c == 0), stop=(c == NC16 - 1)),
                      [sk, "ones_f"], [pk])
        self.act(rstd[:, :], pt[:, :], AF.Sqrt, [pk, "epsb"], ["rstd"], scale=1.0 / D, bias=self.epsb[:, 0:1])
        self.P.op("vector", lambda e: e.reciprocal(out=rstd[:, :], in_=rstd[:, :]), ["rstd"], ["rstd"])
        for c in range(NC16):
            self.stt(hT[:, c, :], xt[:, c, :], gcol[:, c:c + 1], rstd[:, :], ALU.mult, ALU.mult, [(xkey, c), "rstd", "gains"], [hkey])

    def ffn(self, l, which, src_x, dst_x):
        nc, P = self.nc, self.P
        wg = self.w[f"ffn{which}_wg"][l]
        wu = self.w[f"ffn{which}_wu"][l]
        wd = self.w[f"ffn{which}_wd"][l]
        gcol = self.gains[:, (l * 3 + (0 if which == 1 else 2)) * 16:(l * 3 + (0 if which == 1 else 2)) * 16 + 16]
        with contextlib.ExitStack() as st:
            xt = self.tile(st, "f_xt", [128, NC16, TT], F32)
            hT = self.tile(st, "f_hT", [128, NC16, TT], BF16)
            aT = self.tile(st, "f_aT", [128, NF, TT], BF16)
            sq = [self.tile(st, f"f_sq{i}", [128, TT], F32) for i in range(2)]
            rstd = self.tile(st, "f_rstd", [128, TT], F32)
            sg = [self.tile(st, f"f_sg{i}", [128, TT], F32) for i in range(2)]
            wgb = [self.tile(st, f"f_wg{i}", [128, NC16, 128], BF16) for i in range(2)]
            wub = [self.tile(st, f"f_wu{i}", [128, NC16, 128], BF16) for i in range(2)]
            wdb = [self.tile(st, f"f_wd{i}", [128, NF, 128], BF16) for i in range(2)]
            for tt in range(NTT):
                tsl = slice(tt * TT, (tt + 1) * TT)
                xkeys = [("f_xt", c) for c in range(NC16)]
                self.dma(xt[:, :, :], src_x[:, :, tsl].rearrange("c p t -> p c t"), "f_xt", ["xs"], xkeys)
                self.rmsnorm_tile(xt, "f_xt", gcol, hT, "f_hT", sq, rstd, l)
                for f in range(NF):
                    b = f % 2
                    P.dma("gpsimd", lambda e, f=f, b=b: e.dma_start(out=wgb[b][:, :, :], in_=wg[f]), f"f_wg{b}", [], [("wg", b)])
                    P.dma("gpsimd", lambda e, f=f, b=b: e.dma_start(out=wub[b][:, :, :], in_=wu[f]), f"f_wu{b}", [], [("wu", b)])
                    pg, pgk = self.ps()
                    pu, puk = self.ps()
                    self.mm(pg[:, :], [(wgb[b][:, c, :], hT[:, c, :]) for c in range(NC16)], [("wg", b), "f_hT"], [pgk])
                    self.mm(pu[:, :], [(wub[b][:, c, :], hT[:, c, :]) for c in range(NC16)], [("wu", b), "f_hT"], [puk])
                    self.act(sg[b][:, :], pg[:, :], AF.Silu, [pgk], [("sg", b)])
                    self.tt(aT[:, f, :], sg[b][:, :], pu[:, :], ALU.mult, [("sg", b), puk], [("f_aT", f)])
                for dc in range(NC16):
                    b = dc % 2
                    P.dma("gpsimd", lambda e, dc=dc, b=b: e.dma_start(out=wdb[b][:, :, :], in_=wd[dc]), f"f_wd{b}", [], [("wd", b)])
                    py, pyk = self.ps()
                    self.mm(py[:, :], [(wdb[b][:, f, :], aT[:, f, :]) for f in range(NF)], [("wd", b)] + [("f_aT", f) for f in range(NF)], [pyk])
                    self.stt(xt[:, dc, :], py[:, :], 0.5, xt[:, dc, :], ALU.mult, ALU.add, [pyk, ("f_xt", dc)], [("f_xt", dc)])
                self.dma(dst_x[:, :, tsl].rearrange("c p t -> p c t"), xt[:, :, :], "f_xo", xkeys, ["xs"])
            P.barrier()

    def final_norm(self, src_x, out):
        with contextlib.ExitStack() as st:
            xt = self.tile(st, "n_xt", [128, NC16, TT], F32)
            ot = self.tile(st, "n_ot", [128, NC16, TT], F32)
            sq = [self.tile(st, f"n_sq{i}", [128, TT], F32) for i in range(2)]
            rstd = self.tile(st, "n_rstd", [128, TT], F32)
            gcol = self.gains[:, L * 3 * 16:L * 3 * 16 + 16]
            for tt in range(NTT):
                tsl = slice(tt * TT, (tt + 1) * TT)
                self.dma(xt[:, :, :], src_x[:, :, tsl].rearrange("c p t -> p c t"), "n_xt", ["xs"], [("n_xt", c) for c in range(NC16)])
                self.rmsnorm_tile(xt, "n_xt", gcol, ot, "n_ot", sq, rstd, 0)
                self.dma(out[:, :, tsl].rearrange("c p t -> p c t"), ot[:, :, :], "n_ot", ["n_ot"], ["out", "n_ot"])
            self.P.barrier()


    def wblk(self, l, blk):
        i = self.wbn % len(self.wb)
        self.wbn += 1
        t = self.wb[i]
        key = ("wb", i)
        src = self.w_in[l][blk]
        self.P.dma("gpsimd", lambda e, t=t, src=src: e.dma_start(out=t[:, :, :], in_=src), f"wb{i}", [], [key])
        return t, key

    def proj_fm(self, wt, wkey, t0, n):
        p, pk = self.ps()
        self.mm(p[:, 0:n], [(wt[:, c, :], self.hTm[:, c, t0:t0 + n]) for c in range(16)], [wkey, "hTm"], [pk])
        return p, pk

    def proj_tm(self, wt, wkey, t0, ncols):
        p, pk = self.ps()
        self.mm(p[:, 0:ncols], [(self.hTm[:, c, t0:t0 + 128], wt[:, c, 0:ncols]) for c in range(16)], [wkey, "hTm"], [pk])
        return p, pk

    def C(self, i):
        return self.cst[:, i * 128:(i + 1) * 128]

    def mix_prep(self, l, src_x, st):
        P = self.P
        gcol = self.gains[:, (l * 3 + 1) * 16:(l * 3 + 1) * 16 + 16]
        with contextlib.ExitStack() as s2:
            xt = self.tile(s2, "m_xt", [128, NC16, TT], F32)
            sq = [self.tile(s2, f"m_sq{i}", [128, TT], F32) for i in range(2)]
            rstd = self.tile(s2, "m_rstd", [128, TT], F32)
            wsm = self.tile(s2, "m_wsm", [128, NC16, 40], BF16)
            for tt in range(NTT):
                tsl = slice(tt * TT, (tt + 1) * TT)
                self.dma(xt[:, :, :], src_x[:, :, tsl].rearrange("c p t -> p c t"), "m_xt", ["xs"], [("m_xt", c) for c in range(NC16)])
                self.rmsnorm_tile(xt, "m_xt", gcol, self.hTm[:, :, tsl], "hTm", sq, rstd, l)
            P.dma("gpsimd", lambda e: e.dma_start(out=wsm[:, :, :], in_=self.w_sm[l]), "m_wsm", [], ["m_wsm"])
            sm = self.sm
            for ti in range(16):
                p, pk = self.proj_tm(wsm, "m_wsm", ti * 128, 40)
                self.copy(sm[:, ti, :], p[:, 0:40], [pk], ["sm"])
            self.act(sm[:, :, 0:8], sm[:, :, 0:8], AF.Sigmoid, ["sm"], ["sm"])
            self.act(sm[:, :, 16:40], sm[:, :, 16:40], AF.Sigmoid, ["sm"], ["sm"])
            gp = self.gpar
            xg = self.tile(s2, "m_xg", [128, 16, 8], F32)
            ax = self.tile(s2, "m_ax", [128, 16, 8], F32)
            nea = self.tile(s2, "m_nea", [128, 16, 8], F32)
            self.tt(xg[:, :, :], sm[:, :, 8:16], gp[:, l, 0, :, :], ALU.add, ["sm", "gpar"], ["m_xg"])
            self.ts(ax[:, :, :], xg[:, :, :], -1.0, ALU.mult, ["m_xg"], ["m_ax"])
            self.tt(ax[:, :, :], ax[:, :, :], xg[:, :, :], ALU.max, ["m_ax", "m_xg"], ["m_ax"])
            self.act(ax[:, :, :], ax[:, :, :], AF.Exp, ["m_ax"], ["m_ax"], scale=-1.0)
            self.act(ax[:, :, :], ax[:, :, :], AF.Ln, ["m_ax", "oneb"], ["m_ax"], bias=self.oneb[:, 0:1])
            self.stt(xg[:, :, :], xg[:, :, :], 0.0, ax[:, :, :], ALU.max, ALU.add, ["m_xg", "m_ax"], ["m_xg"])
            self.act(nea[:, :, :], gp[:, l, 1, :, :], AF.Exp, ["gpar"], ["m_nea"])
            self.stt(self.gg[:, :, :], xg[:, :, :], -1.0, nea[:, :, :], ALU.mult, ALU.mult, ["m_xg", "m_nea"], ["gg"])
            for ti in range(16):
                p, pk = self.ps()
                def fn(e, p=p, ti=ti, gg_=self.gg):
                    e.matmul(p[:, 0:8], lhsT=self.C(1), rhs=gg_[:, ti, :], start=True, stop=True)
                    return e.matmul(p[:, 8:16], lhsT=self.C(4), rhs=gg_[:, ti, :], start=True, stop=True)
                P.op("tensor", fn, ["gg", "cst"], [pk])
                self.copy(self.gcl[:, ti, :], p[:, 0:16], [pk], ["gcl"])
            self.act(self.egcl[:, :, :], self.gcl[:, :, :], AF.Exp, ["gcl"], ["egcl"])
            P.barrier()

    def hgrn(self, l, ys):
        P = self.P
        H = 1024
        with contextlib.ExitStack() as st:
            B = {n: self.tile(st, "h_" + n, [128, H], F32) for n in ("q", "k", "g", "t", "b", "qd", "kd", "qe", "ks", "o")}
            vtm = self.tile(st, "h_vtm", [128, 8, 128], F32)
            ATs = [self.tile(st, f"h_AT{i}", [128, 128], F32) for i in range(2)]
            kstms = [self.tile(st, f"h_kstm{i}", [128, 128], F32) for i in range(2)]
            state = self.tile(st, "h_state", [128, 128], F32)
            dec = self.tile(st, "h_dec", [128, 16], F32)
            zT = self.tile(st, "h_zT", [128, H], BF16)
            sqb = self.tile(st, "h_sqb", [128, 512], F32)
            rmask = self.tile(st, "h_rmask", [128, H], F32)
            P.op("vector", lambda e: e.memset(rmask[:, :], 1.0), [], ["h_rmask"])
            P.op("vector", lambda e: e.memset(rmask[:, :].rearrange("p (n c) -> p n c", c=64)[:, :, 0:1], 0.0), ["h_rmask"], ["h_rmask"])
            P.op("vector", lambda e, sq_=self.ssqC: e.memset(sq_[:, :], 0.0), [], ["ssqC"])
            v3 = lambda t: t[:, :].rearrange("p (n c) -> p n c", c=64)
            for h in range(8):
                P.op("vector", lambda e: e.memset(state[:, :], 0.0), ["h_state"], ["h_state"])
                lbc = self.lb[:, l * 8 + h:l * 8 + h + 1]
                omlc = self.oml[:, l * 8 + h:l * 8 + h + 1]
                for half in range(2):
                    T0 = half * H
                    if half == 0:
                        if h == 0:
                            hnext = [self.wblk(l, blk) for blk in (52, 60, 68, 76)]
                        (wq, wqk), (wf, wfk), (wi, wik), (wg_, wgk) = hnext
                    for tb in range(2):
                        cs = slice(tb * 512, (tb + 1) * 512)
                        p, pk = self.proj_fm(wq, wqk, T0 + tb * 512, 512)
                        self.act(B["q"][:, cs], p[:, :], AF.Silu, [pk], ["h_q"])
                        p, pk = self.proj_fm(wf, wfk, T0 + tb * 512, 512)
                        self.act(B["k"][:, cs], p[:, :], AF.Sigmoid, [pk], ["h_k"])
                        p, pk = self.proj_fm(wg_, wgk, T0 + tb * 512, 512)
                        self.act(B["g"][:, cs], p[:, :], AF.Silu, [pk], ["h_g"])
                    for j in range(8):
                        p, pk = self.proj_tm(wi, wik, T0 + j * 128, 128)
                        self.copy(vtm[:, j, :], p[:, 0:128], [pk], ["h_vtm"], eng="scalar" if False else "vector")
                    if half == 1 and h < 7:
                        hnext = [self.wblk(l, blk + h + 1) for blk in (52, 60, 68, 76)]
                    self.ts(B["k"][:, :], B["k"][:, :], omlc, ALU.mult, ["h_k", "lb"], ["h_k"], s2=lbc, op1=ALU.add)
                    self.ts(B["t"][:, :], B["k"][:, :], 1e-30, ALU.max, ["h_k"], ["h_t"])
                    self.act(B["t"][:, :], B["t"][:, :], AF.Ln, ["h_t"], ["h_t"])
                    self.ts(B["k"][:, :], B["k"][:, :], -1.0, ALU.mult, ["h_k"], ["h_k"], s2=1.0, op1=ALU.add)
                    P.op("vector", lambda e: e.tensor_tensor_scan(out=B["b"][:, :], data0=rmask[:, :], data1=B["t"][:, :], initial=0.0, op0=ALU.mult, op1=ALU.add),
                         ["h_t", "h_rmask"], ["h_b"])
                    b3 = v3(B["b"])
                    self.act(B["qe"][:, :], B["b"][:, :], AF.Exp, ["h_b"], ["h_qe"])
                    self.act(dec[:, :], b3[:, :, 63], AF.Exp, ["h_b"], ["h_dec"])
                    self.tt(v3(B["t"]), b3, b3[:, :, 32:33].to_broadcast([128, 16, 64]), ALU.subtract, ["h_b"], ["h_t"])
                    self.act(B["qd"][:, :], B["t"][:, :], AF.Exp, ["h_t"], ["h_qd"])
                    self.act(B["kd"][:, :], B["t"][:, :], AF.Exp, ["h_t"], ["h_kd"], scale=-1.0)
                    self.tt(B["qe"][:, :], B["qe"][:, :], B["q"][:, :], ALU.mult, ["h_qe", "h_q"], ["h_qe"])
                    self.tt(B["qd"][:, :], B["qd"][:, :], B["q"][:, :], ALU.mult, ["h_qd", "h_q"], ["h_qd"])
                    self.tt(B["kd"][:, :], B["kd"][:, :], B["k"][:, :], ALU.mult, ["h_kd", "h_k"], ["h_kd"])
                    self.tt(v3(B["t"]), b3[:, :, 63:64].to_broadcast([128, 16, 64]), b3, ALU.subtract, ["h_b"], ["h_t"])
                    self.act(B["ks"][:, :], B["t"][:, :], AF.Exp, ["h_t"], ["h_ks"])
                    self.tt(B["ks"][:, :], B["ks"][:, :], B["k"][:, :], ALU.mult, ["h_ks", "h_k"], ["h_ks"])
                    for j in range(8):
                        js = slice(j * 128, (j + 1) * 128)
                        AT = ATs[j % 2]
                        kstm = kstms[j % 2]
                        pa, pak = self.ps()
                        self.mm(pa[:, 0:128], [(B["kd"][:, js], B["qd"][:, js])], ["h_kd", "h_qd"], [pak])
                        self.tt(AT[:, :], pa[:, 0:128], self.C(1), ALU.mult, [pak, "cst"], [("h_AT", j % 2)])
                        pt, ptk = self.ps()
                        P.op("tensor", lambda e, pt=pt, js=js: e.transpose(pt[:, 0:128], B["ks"][:, js], self.C(0)), ["h_ks", "cst"], [ptk])
                        self.copy(kstm[:, :], pt[:, 0:128], [ptk], [("h_kstm", j % 2)])
                        psns = []
                        for cc in range(2):
                            rs = slice(cc * 64, (cc + 1) * 64)
                            psn, psk = self.ps()
                            P.op("tensor", lambda e, psn=psn, rs=rs, j=j, kstm=kstm: e.matmul(psn[:, 0:128], lhsT=kstm[rs, :], rhs=vtm[rs, j, :], start=True, stop=True),
                                 [("h_kstm", j % 2), "h_vtm"], [psk])
                            psns.append((psn, psk))
                        po, pok = self.ps()
                        for cc in range(2):
                            rs = slice(cc * 64, (cc + 1) * 64)
                            cols = slice(j * 128 + cc * 64, j * 128 + cc * 64 + 64)
                            def fnp(e, po=po, rs=rs, cols=cols, j=j, AT=AT):
                                e.matmul(po[:, rs], lhsT=vtm[:, j, :], rhs=AT[:, rs], start=True, stop=False)
                                return e.matmul(po[:, rs], lhsT=state[:, :], rhs=B["qe"][:, cols], start=False, stop=True)
                            P.op("tensor", fnp, ["h_vtm", ("h_AT", j % 2), "h_state", "h_qe", pok], [pok])
                            psn, psk = psns[cc]
                            self.stt(state[:, :], state[:, :], dec[:, j * 2 + cc:j * 2 + cc + 1], psn[:, 0:128], ALU.mult, ALU.add, ["h_state", "h_dec", psk], ["h_state"])
                        self.copy(B["o"][:, js], po[:, 0:128], [pok], ["h_o"], eng="vector")
                    for tb in range(2):
                        cs = slice(tb * 512, (tb + 1) * 512)
                        self.act(sqb[:, :], B["o"][:, cs], AF.Square, ["h_o"], ["h_sqb"])
                        p, pk = self.ps()
                        self.mm(p[:, :], [(self.ones_f[:, :], sqb[:, :])], ["h_sqb", "ones_f"], [pk])
                        gs = slice(T0 + tb * 512, T0 + (tb + 1) * 512)
                        self.tt(self.ssqC[:, gs], self.ssqC[:, gs], p[:, :], ALU.add, [pk, "ssqC"], ["ssqC"])
                    self.stt(zT[:, :], B["o"][:, :], self.hgn[:, l * 8 + h:l * 8 + h + 1], B["g"][:, :], ALU.mult, ALU.mult, ["h_o", "h_g", "hgn"], ["h_zT"])
                    self.dma(ys[2, h, :, T0:T0 + H], zT[:, :], "h_zT", ["h_zT"], ["ys"])
            self.act(self.ssqC[:, :], self.ssqC[:, :], AF.Sqrt, ["ssqC", "epsb"], ["ssqC"], scale=1.0 / 1024.0, bias=self.epsb[:, 0:1])
            P.op("vector", lambda e, sq_=self.ssqC: e.reciprocal(out=sq_[:, :], in_=sq_[:, :]), ["ssqC"], ["ssqC"])
            P.barrier()

    def gdn(self, l, ys):
        P = self.P
        H = 1024
        with contextlib.ExitStack() as st:
            B = {n: self.tile(st, "g_" + n, [128, H], F32) for n in ("q", "k", "v", "g", "o")}
            stage = [self.tile(st, f"g_st{i}", [128, 515], F32) for i in range(3)]
            sq = self.tile(st, "g_sq", [128, 512], F32)
            rs_ = self.tile(st, "g_rs", [128, 512], F32)
            yT = self.tile(st, "g_yT", [128, 512], BF16)
            names = ("grep", "brep", "dec", "decT", "egcB", "A0", "A0T", "X1", "XT1", "X2", "XT2", "attnT", "Tt", "R0v", "R0w", "kdec", "u", "wTA", "wTB", "qg", "vnew", "state")
            import os
            GW = int(os.environ.get('GW', '4'))
            GPE = os.environ.get('GPE', 'gpsimd')
            RN = ("A0", "A0T", "X1", "XT1", "X2", "XT2", "Tt")
            MS = [{n: self.tile(st, f"g{w}_" + n, [128, 128], F32R if n in RN else F32) for n in names if n != "state"} for w in range(GW)]
            V = lambda ap: ap.bitcast(F32)
            state = self.tile(st, "g_state", [128, 128], F32)
            M = {"state": state}
            for w in range(GW):
                P.op("vector", lambda e, w=w: e.memset(MS[w]["wTA"][:, :], 0.0), [], [f"g{w}_wTA"])
                P.op("vector", lambda e, w=w: e.memset(MS[w]["wTB"][:, :], 0.0), [], [f"g{w}_wTB"])
            for h in range(8):
                P.op("vector", lambda e: e.memset(M["state"][:, :], 0.0), ["g_state"], ["g_state"])
                for i in range(3):
                    P.op("vector", lambda e, i=i: e.memset(stage[i][:, 0:3], 0.0), [("g_st", i)], [("g_st", i)])
                for half in range(2):
                    T0 = half * H
                    if half == 0:
                        if h == 0:
                            gnext = [self.wblk(l, blk) for blk in (0, 8, 16, 24)]
                        wts = gnext[0:3]
                        wg_, wgk = gnext[3]
                    for tb in range(2):
                        cs = slice(tb * 512, (tb + 1) * 512)
                        for i, nm in enumerate(("q", "k", "v")):
                            cw = lambda tap, i=i: self.gcw[:, l, i * 8 + h, tap:tap + 1]
                            p, pk = self.proj_fm(wts[i][0], wts[i][1], T0 + tb * 512, 512)
                            sk = ("g_st", i)
                            self.copy(stage[i][:, 3:515], p[:, :], [pk], [sk])
                            dst = B[nm][:, cs]
                            dk_ = "g_" + nm
                            self.ts(dst, stage[i][:, 3:515], cw(3), ALU.mult, [sk, "gcw"], [dk_])
                            for tap in (2, 1, 0):
                                self.stt(dst, stage[i][:, tap:tap + 512], cw(tap), dst, ALU.mult, ALU.add, [sk, "gcw", dk_], [dk_])
                            self.copy(stage[i][:, 0:3], stage[i][:, 512:515], [sk], [sk])
                            self.act(dst, dst, AF.Silu, [dk_], [dk_])
                        nrm = []
                        for i, nm in enumerate(("q", "k")):
                            dst = B[nm][:, cs]
                            dk_ = "g_" + nm
                            sk = ("g_st", i)
                            sqv = stage[i][:, 3:515]
                            rsv, rsk = (rs_, "g_rs") if i == 0 else (sq, "g_sq")
                            self.act(sqv, dst, AF.Square, [dk_], [sk])
                            pn, pnk = self.ps()
                            self.mm(pn[:, :], [(self.ones_f[:, :], sqv)], [sk, "ones_f"], [pnk])
                            self.act(rsv[:, :], pn[:, :], AF.Sqrt, [pnk, "epsb"], [rsk], bias=self.epsb[:, 0:1])
                            nrm.append((dst, dk_, rsv, rsk, nm))
                        for dst, dk_, rsv, rsk, nm in nrm:
                            P.op("vector", lambda e, rsv=rsv: e.reciprocal(out=rsv[:, :], in_=rsv[:, :]), [rsk], [rsk])
                            self.stt(dst, dst, (128.0 ** -0.5) if nm == "q" else 1.0, rsv[:, :], ALU.mult, ALU.mult, [dk_, rsk], [dk_])
                        p, pk = self.proj_fm(wg_, wgk, T0 + tb * 512, 512)
                        self.act(B["g"][:, cs], p[:, :], AF.Silu, [pk], ["g_g"])
                    if half == 1 and h < 7:
                        gnext = [self.wblk(l, blk + h + 1) for blk in (0, 8, 16, 24)]
                    rec_done = [-1]

                    def tile_job(j, w, half=half, h=h):
                        M = MS[w]
                        T = lambda n: f"g{w}_" + n
                        ti = half * 8 + j
                        js = slice(j * 128, (j + 1) * 128)
                        bcol = self.sm[:, ti, h:h + 1]
                        gcol_ = self.gg[:, ti, h:h + 1]
                        gccol = self.gcl[:, ti, h:h + 1]
                        egccol = self.egcl[:, ti, h:h + 1]
                        eglcol = self.egcl[:, ti, 8 + h:9 + h]
                        self.ts(M["grep"][:, :], self.ones_f[:, :], gcol_, ALU.mult, ["ones_f", "gg"], [T("grep")])
                        self.ts(M["brep"][:, :], self.ones_f[:, :], bcol, ALU.mult, ["ones_f", "sm"], [T("brep")])
                        yield
                        pb, pbk = self.psh(w, 0)
                        def fn(e, pb=pb):
                            e.matmul(pb[:, 0:128], lhsT=M["grep"][:, :], rhs=self.C(1), start=True, stop=True)
                            return e.matmul(pb[:, 128:256], lhsT=M["brep"][:, :], rhs=self.C(0), start=True, stop=True)
                        P.op("tensor", fn, [T("grep"), T("brep"), "cst"], [pbk])
                        pk2, pk2k = self.psh(w, 1)
                        def fn2(e, pk2=pk2, js=js):
                            e.matmul(pk2[:, 0:128], lhsT=B["k"][:, js], rhs=B["k"][:, js], start=True, stop=True)
                            return e.matmul(pk2[:, 128:256], lhsT=B["k"][:, js], rhs=B["q"][:, js], start=True, stop=True)
                        P.op("tensor", fn2, ["g_k", "g_q"], [pk2k])
                        ptk_, ptkk = self.psh(w, 2)
                        def fn3(e, ptk_=ptk_, js=js):
                            e.transpose(ptk_[:, 0:128], B["k"][:, js], self.C(0))
                            return e.transpose(ptk_[:, 128:256], B["v"][:, js], self.C(0))
                        P.op("tensor", fn3, ["g_k", "g_v", "cst"], [ptkk])
                        yield
                        self.ts(M["dec"][:, :], pb[:, 0:128], gccol, ALU.subtract, [pbk, "gcl"], [T("dec")], s2=0.0, op1=ALU.max)
                        self.act(M["dec"][:, :], M["dec"][:, :], AF.Exp, [T("dec")], [T("dec")], scale=-1.0)
                        yield
                        self.ts(M["decT"][:, :], pb[:, 0:128], gccol, ALU.subtract, [pbk, "gcl"], [T("decT")], s2=0.0, op1=ALU.min)
                        self.act(M["decT"][:, :], M["decT"][:, :], AF.Exp, [T("decT")], [T("decT")])
                        self.act(M["egcB"][:, :], pb[:, 0:128], AF.Exp, [pbk], [T("egcB")])
                        yield
                        self.stt(M["A0T"][:, :], pk2[:, 0:128], bcol, M["dec"][:, :], ALU.mult, ALU.mult, [pk2k, "sm", T("dec")], [T("A0T")])
                        yield
                        self.tt(M["A0T"][:, :], V(M["A0T"][:, :]), self.C(4), ALU.mult, [T("A0T"), "cst"], [T("A0T")], eng=GPE)
                        self.tt(M["A0"][:, :], pk2[:, 0:128], M["decT"][:, :], ALU.mult, [pk2k, T("decT")], [T("A0")])
                        yield
                        self.tt(M["A0"][:, :], V(M["A0"][:, :]), pb[:, 128:256], ALU.mult, [T("A0"), pbk], [T("A0")])
                        yield
                        self.tt(M["A0"][:, :], V(M["A0"][:, :]), self.C(2), ALU.mult, [T("A0"), "cst"], [T("A0")], eng=GPE)
                        self.tt(M["attnT"][:, :], pk2[:, 128:256], M["decT"][:, :], ALU.mult, [pk2k, T("decT")], [T("attnT")])
                        yield
                        self.tt(M["attnT"][:, :], M["attnT"][:, :], self.C(1), ALU.mult, [T("attnT"), "cst"], [T("attnT")], eng=GPE)
                        self.ts(M["R0v"][:, :], ptk_[:, 128:256], bcol, ALU.mult, [ptkk, "sm"], [T("R0v")])
                        yield
                        self.ts(M["R0w"][:, :], ptk_[:, 0:128], bcol, ALU.mult, [ptkk, "sm", "egcl"], [T("R0w")], s2=egccol, op1=ALU.mult)
                        yield
                        self.ts(M["kdec"][:, :], ptk_[:, 0:128], eglcol, ALU.mult, [ptkk, "egcl"], [T("kdec")])
                        self.tt(M["Tt"][:, :], self.C(0), V(M["A0"][:, :]), ALU.subtract, ["cst", T("A0")], [T("Tt")], eng=GPE)
                        self.tt(M["qg"][:, :], B["q"][:, js], M["egcB"][:, :], ALU.mult, ["g_q", T("egcB")], [T("qg")], eng=GPE)
                        yield
                        X, XT, Xn, XTn = "A0", "A0T", "X1", "XT1"
                        for lev in range(5):
                            px, pxk = self.psh(w, lev % 2)
                            def fn4(e, px=px, X=X, XT=XT):
                                e.matmul(px[:, 0:128], lhsT=M[XT][:, :], rhs=M[X][:, :], start=True, stop=True)
                                return e.matmul(px[:, 128:256], lhsT=M[X][:, :], rhs=M[XT][:, :], start=True, stop=True)
                            P.op("tensor", fn4, [T(X), T(XT)], [pxk])
                            yield
                            self.copy(M[Xn][:, :], px[:, 0:128], [pxk], [T(Xn)])
                            self.act(M[XTn][:, :], px[:, 128:256], AF.Identity, [pxk], [T(XTn)])
                            yield
                            X, XT = Xn, XTn
                            Xn, XTn = ("X2", "XT2") if X == "X1" else ("X1", "XT1")
                            pa, pak = self.psh(w, 2 + lev % 2)
                            self.mm(pa[:, 0:128], [(M[XT][:, :], M["Tt"][:, :])], [T(XT), T("Tt")], [pak])
                            yield
                            self.tt(M["Tt"][:, :], V(M["Tt"][:, :]), pa[:, 0:128], ALU.add, [T("Tt"), pak], [T("Tt")])
                            yield
                        pu, puk = self.psh(w, 1)
                        def fn5(e, pu=pu):
                            e.matmul(pu[:, 0:128], lhsT=V(M["Tt"][:, :]), rhs=M["R0v"][:, :], start=True, stop=True)
                            return e.matmul(pu[:, 128:256], lhsT=M["R0w"][:, :], rhs=V(M["Tt"][:, :]), start=True, stop=True)
                        P.op("tensor", fn5, [T("Tt"), T("R0v"), T("R0w")], [puk])
                        yield
                        self.copy(M["u"][:, :], pu[:, 0:128], [puk], [T("u")])
                        self.act(M["wTA"][:, 0:64], pu[:, 128:192], AF.Identity, [puk], [T("wTA")])
                        self.act(M["wTB"][:, 64:128], pu[:, 192:256], AF.Identity, [puk], [T("wTB")])
                        yield
                        while rec_done[0] != j - 1:
                            yield
                        po, pok = self.psh(w, 0)
                        for cc in range(2):
                            rs = slice(cc * 64, (cc + 1) * 64)
                            wn = "wTA" if cc == 0 else "wTB"
                            pv, pvk = self.psh(w, 2)
                            self.mm(pv[:, 0:128], [(M[wn][:, :], state[:, :])], [T(wn), "g_state"], [pvk])
                            yield
                            self.tt(M["vnew"][rs, :], M["u"][rs, :], pv[rs, 0:128], ALU.subtract, [T("u"), pvk], [T("vnew")])
                            yield
                            def fn6(e, po=po, rs=rs):
                                e.matmul(po[:, rs], lhsT=state[:, :], rhs=M["qg"][:, rs], start=True, stop=False)
                                return e.matmul(po[:, rs], lhsT=M["vnew"][rs, :], rhs=M["attnT"][rs, rs], start=False, stop=True)
                            P.op("tensor", fn6, ["g_state", T("qg"), T("vnew"), T("attnT"), pok], [pok])
                            psn, psk = self.psh(w, 3)
                            self.mm(psn[:, 0:128], [(M["kdec"][rs, :], M["vnew"][rs, :])], [T("kdec"), T("vnew")], [psk])
                            yield
                            ecol = M["egcB"][:, cc * 64 + 63:cc * 64 + 64]
                            self.stt(state[:, :], state[:, :], ecol, psn[:, 0:128], ALU.mult, ALU.add, ["g_state", T("egcB"), psk], ["g_state"] if os.environ.get("GREC", "1") == "1" else [T("fake")])
                            yield
                        self.copy(B["o"][:, js], po[:, 0:128], [pok], ["g_o"])
                        rec_done[0] = j
                        yield

                    self.interleave([(lambda w, j=j: tile_job(j, w)) for j in range(8)], GW)
                    for tb in range(2):
                        cs = slice(tb * 512, (tb + 1) * 512)
                        self.act(sq[:, :], B["o"][:, cs], AF.Square, ["g_o"], ["g_sq"])
                        pn, pnk = self.ps()
                        self.mm(pn[:, :], [(self.ones_f[:, :], sq[:, :])], ["g_sq", "ones_f"], [pnk])
                        self.act(rs_[:, :], pn[:, :], AF.Sqrt, [pnk, "epsb"], ["g_rs"], scale=1.0 / 128.0, bias=self.epsb[:, 0:1])
                        P.op("vector", lambda e: e.reciprocal(out=rs_[:, :], in_=rs_[:, :]), ["g_rs"], ["g_rs"])
                        self.stt(sq[:, :], B["o"][:, cs], self.gdn_n[:, l:l + 1], rs_[:, :], ALU.mult, ALU.mult, ["g_o", "g_rs", "gdn_n"], ["g_sq"])
                        self.tt(yT[:, :], sq[:, :], B["g"][:, cs], ALU.mult, ["g_sq", "g_g"], ["g_yT"])
                        self.dma(ys[0, h, :, T0 + tb * 512:T0 + (tb + 1) * 512], yT[:, :], "g_yT", ["g_yT"], ["ys"])
            P.barrier()

    def nsa(self, l, ys):
        P = self.P
        SC = 128.0 ** -0.5
        with contextlib.ExitStack() as st:
            qT = self.tile(st, "n_qT", [128, 4, S], BF16)
            kslc = self.tile(st, "n_kslc", [128, S], BF16)
            kwin = self.tile(st, "n_kwin", [128, S], BF16)
            vslc = self.tile(st, "n_vslc", [128, 16, 129], BF16)
            vwin = self.tile(st, "n_vwin", [128, 16, 129], BF16)
            kcT = self.tile(st, "n_kcT", [128, 128], BF16)
            vcA = self.tile(st, "n_vcA", [128, 161], BF16)
            vmask = self.tile(st, "n_vmask", [128, S], BF16)
            emat = self.tile(st, "n_emat", [32, 16, 128], BF16)
            amk = self.tile(st, "n_amk", [128, 16, 32], F32)
            bmk = self.tile(st, "n_bmk", [128, 16, 32], F32)
            w2 = self.tile(st, "n_w2", [128, 2, 128], BF16)
            peT = self.tile(st, "n_peT", [128, 2, 32], F32)
            P.dma("gpsimd", lambda e: e.dma_start(out=vmask[:, :], in_=self.nsa_c["vmask"]), "n_c0", [], ["n_vmask"])
            P.dma("gpsimd", lambda e: e.dma_start(out=emat[:, :, :], in_=self.nsa_c["emat"]), "n_c1", [], ["n_emat"])
            P.dma("gpsimd", lambda e: e.dma_start(out=vcA[:, 129:161], in_=self.nsa_c["ovl"]), "n_c2", [], ["n_vcA"])
            self.dma(amk[:, :, :], self.nsa_c["amk"], "n_c3", [], ["n_amk"])
            self.dma(bmk[:, :, :], self.nsa_c["bmk"], "n_c4", [], ["n_bmk"])
            P.dma("gpsimd", lambda e: e.dma_start(out=w2[:, :, :], in_=self.nsa_w2[l]), "n_c5", [], ["n_w2"])
            self.dma(peT[:, :, :], self.nsa_pe[l], "n_c6", [], ["n_peT"])
            P.op("vector", lambda e: e.memset(vcA[:, 128:129], 1.0), [], ["n_vcA1"])
            P.op("vector", lambda e: e.memset(vslc[:, :, 128:129], 1.0), [], ["n_vslc"])
            P.op("vector", lambda e: e.memset(vwin[:, :, 128:129], 1.0), [], ["n_vwin"])
            for g in range(2):
                with contextlib.ExitStack() as s2:
                    tT = self.tile(s2, "n_tT", [128, S], F32)
                    tpe = self.tile(s2, "n_tpe", [128, 32, 127], BF16)
                    w1 = self.tile(s2, "n_w1", [128, 32, 128], BF16)
                    hid = self.tile(s2, "n_hid", [128, 128], BF16)
                    for r in range(4):
                        wq, wqk = self.wblk(l, 32 + g * 4 + r)
                        for tb in range(4):
                            p, pk = self.proj_fm(wq, wqk, tb * 512, 512)
                            self.ts(qT[:, r, tb * 512:(tb + 1) * 512], p[:, :], SC, ALU.mult, [pk], ["n_qT"])
                    for nm, dst, slot in (("n_kslc", kslc, 2), ("n_kwin", kwin, 4)):
                        wk, wkk = self.wblk(l, 40 + slot * 2 + g)
                        for tb in range(4):
                            p, pk = self.proj_fm(wk, wkk, tb * 512, 512)
                            self.copy(dst[:, tb * 512:(tb + 1) * 512], p[:, :], [pk], [nm])
                    for nm, dst, slot in (("n_vslc", vslc, 3), ("n_vwin", vwin, 5)):
                        wv, wvk = self.wblk(l, 40 + slot * 2 + g)
                        for ti in range(16):
                            p, pk = self.proj_tm(wv, wvk, ti * 128, 128)
                            self.copy(dst[:, ti, 0:128], p[:, 0:128], [pk], [nm])
                    for kv in range(2):
                        wt, wtk = self.wblk(l, 40 + kv * 2 + g)
                        P.dma("gpsimd", lambda e, kv=kv, w1=w1: e.dma_start(out=w1[:, :, :], in_=self.nsa_w1[l][kv]), "n_w1", [], ["n_w1"])
                        for tb in range(4):
                            p, pk = self.proj_fm(wt, wtk, tb * 512, 512)
                            self.copy(tT[:, tb * 512:(tb + 1) * 512], p[:, :], [pk], ["n_tT"])
                        t3 = tT[:, :].rearrange("p (n s) -> p n s", s=16)
                        for j in range(32):
                            src = t3[:, 0:127, j] if j < 16 else t3[:, 1:128, j - 16]
                            self.ts(tpe[:, j, :], src, peT[:, kv, j:j + 1], ALU.add, ["n_tT", "n_peT"], [("n_tpe", j)])
                        ph, phk = self.ps()
                        self.mm(ph[:, 0:127], [(w1[:, j, :], tpe[:, j, :]) for j in range(32)], ["n_w1"] + [("n_tpe", j) for j in range(32)], [phk])
                        self.act(hid[:, 0:127], ph[:, 0:127], AF.Silu, [phk], ["n_hid"])
                        pc, pck = self.ps()
                        if kv == 0:
                            self.mm(pc[:, 0:127], [(w2[:, 0, :], hid[:, 0:127])], ["n_w2", "n_hid"], [pck])
                            self.copy(kcT[:, 0:127], pc[:, 0:127], [pck], ["n_kcT"])
                        else:
                            self.mm(pc[0:127, 0:128], [(hid[:, 0:127], w2[:, 1, :])], ["n_w2", "n_hid"], [pck])
                            self.copy(vcA[0:127, 0:128], pc[0:127, 0:128], [pck], ["n_vcA"])
                    P.barrier()
                with contextlib.ExitStack() as s3:
                    ybT = self.tile(s3, "n_ybT", [128, 4, S], BF16)
                    pT = [self.tile(s3, f"n_pT{i}", [128, 512], BF16) for i in range(3)]
                    oacc = self.tile(s3, "n_oacc", [128, 4, 128], F32)
                    impa = self.tile(s3, "n_impa", [128, 32], F32)
                    impt = self.tile(s3, "n_impt", [128, 32], F32)
                    top8 = self.tile(s3, "n_top8", [128, 8], F32)
                    rden = self.tile(s3, "n_rden", [128, 4], F32)
                    selT4 = self.tile(s3, "n_selT4", [32, 4, 128], BF16)
                    pTn = [0]
                    self.ps_rng = (0, 4)

                    def attend(qi, br, ktiles):
                        qs = slice(qi * 128, (qi + 1) * 128)
                        ncol = 161 if br == 0 else 129
                        po = [(self.psum[4 + r], ("ps", 4 + r), 0) for r in range(4)]
                        nkt = len(ktiles)
                        def stage1(ki):
                            kT_ap, vA_ap, nk, mmask, post, rk = ktiles[ki]
                            pS, pSk = self.ps()
                            pairs = [(kT_ap, qT[:, :, qs])]
                            if mmask is not None:
                                pairs.append((mmask, selT4[:, :, :]))
                            self.mm(pS[0:nk, :], pairs, ["n_qT", "n_selT4", "n_emat"] + rk, [pSk])
                            pb_ = pT[pTn[0] % 3]
                            pbk_ = ("n_pT", pTn[0] % 3)
                            pTn[0] += 1
                            self.act(pb_[0:nk, :], pS[0:nk, :], AF.Exp, [pSk], [pbk_])
                            if post is not None:
                                v_ = pb_[0:nk, :].rearrange("p (r t) -> p r t", r=4)
                                P.op("gpsimd", lambda e, v_=v_, post=post, nk=nk: e.tensor_tensor(out=v_, in0=v_, in1=post.unsqueeze(1).to_broadcast([nk, 4, 128]), op=ALU.mult),
                                     [pbk_, "cst", "n_vmask"], [pbk_])
                            return pb_, pbk_

                        def stage2(ki, pb_, pbk_):
                            kT_ap, vA_ap, nk, mmask, post, rk = ktiles[ki]
                            for r in range(4):
                                pr, prk, c0 = po[r]
                                P.op("tensor", lambda e, pr=pr, c0=c0, r=r, pb_=pb_, vA_ap=vA_ap, nk=nk, ki=ki: e.matmul(
                                    pr[:, c0:c0 + ncol], lhsT=pb_[0:nk, r * 128:(r + 1) * 128], rhs=vA_ap, start=(ki == 0), stop=(ki == nkt - 1)),
                                    [pbk_, prk] + rk, [prk])

                        prev = stage1(0)
                        for ki in range(1, nkt):
                            cur_ = stage1(ki)
                            stage2(ki - 1, *prev)
                            prev = cur_
                        stage2(nkt - 1, *prev)
                        for r in range(4):
                            pr, prk, c0 = po[r]
                            gc_ = self.sm[:, qi, 16 + (g * 4 + r) * 3 + br:16 + (g * 4 + r) * 3 + br + 1]
                            self.ts(rden[:, r:r + 1], pr[:, c0 + 128:c0 + 129], 1e-30, ALU.max, [prk], ["n_rden"])
                            P.op("vector", lambda e, r=r, rden=rden: e.reciprocal(out=rden[:, r:r + 1], in_=rden[:, r:r + 1]), ["n_rden"], ["n_rden"])
                            if br == 0:
                                if r == 0:
                                    self.ts(impa[:, :], pr[:, c0 + 129:c0 + 161], rden[:, r:r + 1], ALU.mult, [prk, "n_rden"], ["n_impa"])
                                else:
                                    self.stt(impa[:, :], pr[:, c0 + 129:c0 + 161], rden[:, r:r + 1], impa[:, :], ALU.mult, ALU.add, [prk, "n_rden", "n_impa"], ["n_impa"])
                            self.tt(rden[:, r:r + 1], rden[:, r:r + 1], gc_, ALU.mult, ["n_rden", "sm"], ["n_rden"])
                            if br == 0:
                                self.ts(oacc[:, r, :], pr[:, c0:c0 + 128], rden[:, r:r + 1], ALU.mult, [prk, "n_rden"], [("n_oacc", r)])
                            else:
                                self.stt(oacc[:, r, :], pr[:, c0:c0 + 128], rden[:, r:r + 1], oacc[:, r, :], ALU.mult, ALU.add, [prk, "n_rden", ("n_oacc", r)], [("n_oacc", r)])

                    for qi in range(16):
                        qs = slice(qi * 128, (qi + 1) * 128)
                        attend(qi, 0, [(kcT[:, 0:127], vcA[0:127, 0:161], 127, None, vmask[0:127, qs], ["n_kcT", "n_vcA", "n_vcA1"])])
                        self.tt(impt[:, :], impa[:, :], amk[:, qi, :], ALU.mult, ["n_impa", "n_amk"], ["n_impt"])
                        self.tt(impt[:, :], impt[:, :], bmk[:, qi, :], ALU.add, ["n_impt", "n_bmk"], ["n_impt"])
                        P.op("vector", lambda e, top8=top8, impt=impt: e.max(out=top8[:, :], in_=impt[:, :]), ["n_impt"], ["n_top8"])
                        self.ts(impt[:, :], impt[:, :], top8[:, 7:8], ALU.is_ge, ["n_impt", "n_top8"], ["n_impt"])
                        self.ts(impt[:, :], impt[:, :], -1.0, ALU.add, ["n_impt"], ["n_impt"], s2=30000.0, op1=ALU.mult)
                        psl, pslk = self.ps()
                        P.op("tensor", lambda e, psl=psl, impt=impt: e.transpose(psl[0:32, 0:128], impt[:, :], self.C(0)), ["n_impt", "cst"], [pslk])
                        self.copy(selT4[:, :, :], psl[0:32, 0:128].unsqueeze(1).to_broadcast([32, 4, 128]), [pslk], ["n_selT4"])
                        kt = []
                        for kj in range(qi + 1):
                            ks_ = slice(kj * 128, (kj + 1) * 128)
                            kt.append((kslc[:, ks_], vslc[:, kj, :], 128, emat[:, kj, :], self.C(5) if kj == qi else None, ["n_kslc", "n_vslc"]))
                        attend(qi, 1, kt)
                        kt = []
                        for kj in range(max(0, qi - 4), qi + 1):
                            ks_ = slice(kj * 128, (kj + 1) * 128)
                            post = self.C(5) if kj == qi else (self.C(6) if kj == qi - 4 else None)
                            kt.append((kwin[:, ks_], vwin[:, kj, :], 128, None, post, ["n_kwin", "n_vwin"]))
                        attend(qi, 2, kt)
                        for r in range(4):
                            pt, ptk = self.ps()
                            P.op("tensor", lambda e, pt=pt, r=r, oacc=oacc: e.transpose(pt[:, 0:128], oacc[:, r, :], self.C(0)), [("n_oacc", r), "cst"], [ptk])
                            self.copy(ybT[:, r, qs], pt[:, 0:128], [ptk], ["n_ybT"])
                    for r in range(4):
                        self.dma(ys[1, g * 4 + r, :, :], ybT[:, r, :], "n_ybT", ["n_ybT"], ["ys"])
                    self.ps_rng = (0, 8)
                    P.barrier()
            P.barrier()

    def merge(self, l, src_x, dst_x, ys):
        P = self.P
        with contextlib.ExitStack() as st:
            yt = [self.tile(st, f"mg_y{i}", [128, 8, TT], BF16) for i in range(3)]
            mg = self.tile(st, "mg_mg", [128, NC16, TT], BF16)
            sg = [self.tile(st, f"mg_sg{i}", [128, TT], F32) for i in range(3)]
            t0_ = self.tile(st, "mg_t0", [128, TT], F32)
            t1_ = self.tile(st, "mg_t1", [128, TT], F32)
            wp = [[self.tile(st, f"mg_wp{i}_{b}", [128, 8, 128], BF16) for b in range(2)] for i in range(3)]
            xc = [self.tile(st, f"mg_xc{b}", [128, TT], F32) for b in range(2)]
            wb_saved = self.wb
            self.wb = list(self.wb) + [self.tile(st, f"mg_wbx{i}", [128, NC16, 128], BF16) for i in range(4)]
            for tt in range(NTT):
                tsl = slice(tt * TT, (tt + 1) * TT)
                for i in range(3):
                    self.dma(yt[i][:, :, :], ys[i, :, :, tsl].rearrange("h p t -> p h t"), f"mg_y{i}", ["ys"], [("mg_y", i)])
                for dc in range(NC16):
                    b = dc % 2
                    pp = []
                    for i in range(3):
                        P.dma("gpsimd", lambda e, i=i, b=b, dc=dc: e.dma_start(out=wp[i][b][:, :, :], in_=self.w_proj[l][i][dc]), f"mg_wp{i}_{b}", [], [("mg_wp", i, b)])
                        p, pk = self.ps()
                        self.mm(p[:, :], [(wp[i][b][:, jc, :], yt[i][:, jc, :]) for jc in range(8)], [("mg_wp", i, b), ("mg_y", i)], [pk])
                        pp.append((p, pk))
                    for i in range(3):
                        wt, wtk = self.wblk(l, 84 + i * 16 + dc)
                        p, pk = self.proj_fm(wt, wtk, tt * TT, TT)
                        self.act(sg[i][:, :], p[:, :], AF.Sigmoid, [pk], [("mg_sg", i)])
                    self.tt(t0_[:, :], sg[0][:, :], pp[0][0][:, :], ALU.mult, [("mg_sg", 0), pp[0][1]], ["mg_t0"])
                    self.tt(t1_[:, :], sg[1][:, :], pp[1][0][:, :], ALU.mult, [("mg_sg", 1), pp[1][1]], ["mg_t1"])
                    self.tt(sg[2][:, :], sg[2][:, :], pp[2][0][:, :], ALU.mult, [("mg_sg", 2), pp[2][1]], [("mg_sg", 2)])
                    self.tt(t0_[:, :], t0_[:, :], t1_[:, :], ALU.add, ["mg_t0", "mg_t1"], ["mg_t0"])
                    self.tt(t1_[:, :], sg[2][:, :], self.ssqC[:, tsl], ALU.mult, [("mg_sg", 2), "ssqC"], ["mg_t1"])
                    self.tt(mg[:, dc, :], t0_[:, :], t1_[:, :], ALU.add, ["mg_t0", "mg_t1"], [("mg_mg", dc)])
                for dc in range(NC16):
                    b = dc % 2
                    self.dma(xc[b][:, :], src_x[dc, :, tsl], f"mg_xc{b}", ["xs"], [("mg_xc", b)])
                    i = self.wbn % len(self.wb)
                    self.wbn += 1
                    wt, wtk = self.wb[i], ("wb", i)
                    P.dma("gpsimd", lambda e, wt=wt, dc=dc: e.dma_start(out=wt[:, :, :], in_=self.w_out[l][dc]), f"wb{i}", [], [wtk])
                    p, pk = self.ps()
                    self.mm(p[:, :], [(wt[:, c, :], mg[:, c, :]) for c in range(NC16)], [wtk] + [("mg_mg", c) for c in range(NC16)], [pk])
                    self.tt(xc[b][:, :], xc[b][:, :], p[:, :], ALU.add, [("mg_xc", b), pk], [("mg_xc", b)])
                    self.dma(dst_x[dc, :, tsl], xc[b][:, :], f"mg_xo{b}", [("mg_xc", b)], ["xs"])
            P.barrier()
            self.wb = wb_saved

    def build(self, stages=None, dbg=None):
        nc = self.nc
        P = self.P
        self.wbn = 0
        self.xin = self.dram_in("xT", [NC16, 128, S])
        need_ffn = stages is None or any(sg[0] == "ffn" for sg in stages)
        need_mix = stages is None or any(sg[0] != "ffn" for sg in stages)
        self.w = {}
        if need_ffn:
            for which in (1, 2):
                self.w[f"ffn{which}_wg"] = [self.dram_in(f"f{which}g{l}", [NF, 128, NC16, 128]) for l in range(L)]
                self.w[f"ffn{which}_wu"] = [self.dram_in(f"f{which}u{l}", [NF, 128, NC16, 128]) for l in range(L)]
                self.w[f"ffn{which}_wd"] = [self.dram_in(f"f{which}d{l}", [NC16, 128, NF, 128]) for l in range(L)]
        if need_mix:
            self.w_in = [self.dram_in(f"win{l}", [132, 128, NC16, 128]) for l in range(L)]
            self.w_sm = [self.dram_in(f"wsm{l}", [128, NC16, 40]) for l in range(L)]
        gains_d = self.dram_in("gains", [128, (L * 3 + 1) * 16])
        cst_d = self.dram_in("cst", [128, 7 * 128])
        gpar_d = self.dram_in("gpar", [128, L, 2, 16, 8])
        hgz_d = self.dram_in("hgz", [128, L * 8])
        hgn_d = self.dram_in("hgn", [128, L * 8])
        gcw_d = self.dram_in("gcw", [128, L, 24, 4])
        if need_mix:
            self.nsa_c = {"vmask": self.dram_in("n_vmask", [128, S]), "emat": self.dram_in("n_emat", [32, 16, 128]),
                          "ovl": self.dram_in("n_ovl", [128, 32]), "amk": self.dram_in("n_amk", [128, 16, 32]),
                          "bmk": self.dram_in("n_bmk", [128, 16, 32])}
            self.nsa_w1 = [[self.dram_in(f"n_w1_{l}_{kv}", [128, 32, 128]) for kv in range(2)] for l in range(L)]
            self.nsa_w2 = [self.dram_in(f"n_w2_{l}", [128, 2, 128]) for l in range(L)]
            self.nsa_pe = [self.dram_in(f"n_pe_{l}", [128, 2, 32]) for l in range(L)]
            self.w_proj = [[self.dram_in(f"wp{l}_{i}", [16, 128, 8, 128]) for i in range(3)] for l in range(L)]
            self.w_out = [self.dram_in(f"wo{l}", [16, 128, 16, 128]) for l in range(L)]
        gdnn_d = self.dram_in("gdnn", [128, L])
        out = nc.dram_tensor("outT", [NC16, 128, S], F32, kind="ExternalOutput").ap()
        xs = nc.dram_tensor("xs", [NC16, 128, S], F32, kind="Internal").ap()
        ys = nc.dram_tensor("ys", [3, 8, 128, S], BF16, kind="Internal").ap()
        if dbg == "ys":
            ysd = nc.dram_tensor("ysd", [3, 8, 128, S], BF16, kind="ExternalOutput").ap()
        with contextlib.ExitStack() as st:
            self.psum = [st.enter_context(nc.psum_tensor(f"psb{i}", [128, 512], F32)) for i in range(8)]
            self.ones_f = self.tile(st, "ones_f", [128, 128], F32)
            self.epsb = self.tile(st, "epsb", [128, 1], F32)
            self.oneb = self.tile(st, "oneb", [128, 1], F32)
            self.gains = self.tile(st, "gains_sb", [128, (L * 3 + 1) * 16], F32)
            self.cst = self.tile(st, "cst_sb", [128, 7 * 128], F32)
            self.gpar = self.tile(st, "gpar_sb", [128, L, 2, 16, 8], F32)
            self.hgz = self.tile(st, "hgz_sb", [128, L * 8], F32)
            self.hgn = self.tile(st, "hgn_sb", [128, L * 8], F32)
            self.lb = self.tile(st, "lb_sb", [128, L * 8], F32)
            self.gcw = self.tile(st, "gcw_sb", [128, L, 24, 4], F32)
            self.gdn_n = self.tile(st, "gdnn_sb", [128, L], F32)
            self.dma(self.gcw[:, :, :, :], gcw_d[:, :, :, :], "gcw", [], ["gcw"])
            self.dma(self.gdn_n[:, :], gdnn_d[:, :], "gdn_n", [], ["gdn_n"])
            self.oml = self.tile(st, "oml_sb", [128, L * 8], F32)
            P.op("vector", lambda e: e.memset(self.ones_f[:, :], 1.0), [], ["ones_f"])
            P.op("vector", lambda e: e.memset(self.epsb[:, :], EPS), [], ["epsb"])
            P.op("vector", lambda e: e.memset(self.oneb[:, :], 1.0), [], ["oneb"])
            self.dma(self.gains[:, :], gains_d[:, :], "gains", [], ["gains"])
            self.dma(self.cst[:, :], cst_d[:, :], "cst", [], ["cst"])
            self.dma(self.gpar[:, :, :, :, :], gpar_d[:, :, :, :, :], "gpar", [], ["gpar"])
            self.dma(self.hgz[:, :], hgz_d[:, :], "hgz", [], ["hgz"])
            self.dma(self.hgn[:, :], hgn_d[:, :], "hgn", [], ["hgn"])
            self.lower_bounds(st)
            if stages is None:
                stages = []
                for l in range(L):
                    stages += [("ffn", l, 1), ("mix", l), ("ffn", l, 2)]
                stages.append(("final",))
            cur = self.xin
            for sg in stages:
                if sg[0] == "ffn":
                    self.ffn(sg[1], sg[2], cur, xs)
                    cur = xs
                elif sg[0] == "final":
                    self.final_norm(cur, out)
                else:
                    l = sg[1]
                    parts = sg[2] if len(sg) > 2 else ("hgrn", "gdn", "nsa", "merge")
                    with contextlib.ExitStack() as ms:
                        self.hTm = self.tile(ms, "hTm", [128, NC16, S], BF16)
                        self.sm = self.tile(ms, "sm", [128, 16, 40], F32)
                        self.gg = self.tile(ms, "gg", [128, 16, 8], F32)
                        self.gcl = self.tile(ms, "gcl", [128, 16, 16], F32)
                        self.egcl = self.tile(ms, "egcl", [128, 16, 16], F32)
                        self.ssqC = self.tile(ms, "ssqC", [128, S], F32)
                        self.wb = [self.tile(ms, f"wb{i}", [128, NC16, 128], BF16) for i in range(4)]
                        self.mix_prep(l, cur, ms)
                        if "hgrn" in parts:
                            self.hgrn(l, ys)
                        if "gdn" in parts:
                            self.gdn(l, ys)
                        if "nsa" in parts:
                            self.nsa(l, ys)
                        if "merge" in parts:
                            self.merge(l, cur, xs, ys)
                            cur = xs
                        P.barrier()
            if dbg == "ys":
                self.dma(ysd[:, :, :, :], ys[:, :, :, :], "dbg", ["ys"], ["ysd"])
                P.barrier()
            elif dbg == "xs":
                self.dma(out[:, :, :], cur[:, :, :], "dbg", ["xs"], ["out"])
                P.barrier()
            P.emit()
        return nc

    def lower_bounds(self, st):
        P = self.P
        m = self.tile(st, "lb_m", [128, 8], F32)
        e = self.tile(st, "lb_e", [128, L * 8], F32)
        ssum = self.tile(st, "lb_s", [128, 8], F32)
        self.copy(m[:, :], self.hgz[:, 0:8], ["hgz"], ["lb_m"])
        for l in range(1, L):
            self.tt(m[:, :], m[:, :], self.hgz[:, l * 8:(l + 1) * 8], ALU.max, ["lb_m", "hgz"], ["lb_m"])
        for l in range(L):
            self.tt(e[:, l * 8:(l + 1) * 8], self.hgz[:, l * 8:(l + 1) * 8], m[:, :], ALU.subtract, ["hgz", "lb_m"], ["lb_e"])
        self.act(e[:, :], e[:, :], AF.Exp, ["lb_e"], ["lb_e"])
        self.copy(ssum[:, :], e[:, 0:8], ["lb_e"], ["lb_s"])
        for l in range(1, L):
            self.tt(ssum[:, :], ssum[:, :], e[:, l * 8:(l + 1) * 8], ALU.add, ["lb_s", "lb_e"], ["lb_s"])
        P.op("vector", lambda en: en.reciprocal(out=ssum[:, :], in_=ssum[:, :]), ["lb_s"], ["lb_s"])
        for l in range(L):
            self.tt(e[:, l * 8:(l + 1) * 8], e[:, l * 8:(l + 1) * 8], ssum[:, :], ALU.mult, ["lb_e", "lb_s"], ["lb_e"])
        self.copy(self.lb[:, 0:8], e[:, 0:8], ["lb_e"], ["lb"])
        for l in range(1, L):
            self.tt(self.lb[:, l * 8:(l + 1) * 8], self.lb[:, (l - 1) * 8:l * 8], e[:, l * 8:(l + 1) * 8], ALU.add, ["lb", "lb_e"], ["lb"])
        for l in range(L):
            self.tt(self.lb[:, l * 8:(l + 1) * 8], self.lb[:, l * 8:(l + 1) * 8], e[:, 0:8], ALU.subtract, ["lb", "lb_e"], ["lb"])
        self.ts(self.oml[:, :], self.lb[:, :], -1.0, ALU.mult, ["lb"], ["lb"] if False else ["oml"], s2=1.0, op1=ALU.add)


def _colblock(w):
    kd, n = w.shape
    return np.ascontiguousarray(w.reshape(kd // 128, 128, n // 128, 128).transpose(2, 1, 0, 3))


def _consts():
    p = np.arange(128)[:, None]
    f = np.arange(128)[None, :]
    same = (p // 64) == (f // 64)
    mats = [p == f, (p <= f) & same, (p < f) & same, (p >= f) & same, (p > f) & same, p <= f, p > f]
    return np.ascontiguousarray(np.concatenate([m.astype(np.float32) for m in mats], axis=1))


def prep_weights(inp, need_ffn=True, need_mix=True):
    m = {}
    if need_ffn:
        for which in (1, 2):
            for l in range(L):
                m[f"f{which}g{l}"] = _colblock(inp[f"ffn{which}_w_gate"][l])
                m[f"f{which}u{l}"] = _colblock(inp[f"ffn{which}_w_up"][l])
                m[f"f{which}d{l}"] = _colblock(inp[f"ffn{which}_w_down"][l])
    if need_mix:
        for l in range(L):
            w = inp["w_in"][l]
            big = np.concatenate([w[:, 0:3072], w[:, 3088:6672], w[:, 6696:16936]], axis=1)
            m[f"win{l}"] = _colblock(big)
            small = np.concatenate([w[:, 3072:3088], w[:, 6672:6696]], axis=1)
            m[f"wsm{l}"] = np.ascontiguousarray(small.reshape(16, 128, 40).transpose(1, 0, 2))
    gl = []
    for l in range(L):
        for nm in ("ffn1_norm", "mix_norm", "ffn2_norm"):
            gl.append(inp[nm][l].reshape(16, 128).T)
    gl.append(inp["final_norm"].reshape(16, 128).T)
    m["gains"] = np.ascontiguousarray(np.concatenate(gl, axis=1)).astype(np.float32)
    m["cst"] = _consts()
    gpar = np.zeros((128, L, 2, 16, 8), np.float32)
    for l in range(L):
        gpar[:, l, 0] = inp["gdn_dt_bias"][l][None, None, :]
        gpar[:, l, 1] = inp["gdn_a_log"][l][None, None, :]
    m["gpar"] = gpar
    if need_mix:
        t = np.arange(S)
        n = np.arange(128)
        vm = ((16 * n[:, None] + 31) <= t[None, :]) & (n[:, None] < 127)
        m["n_vmask"] = vm.astype(np.float32)
        j = np.arange(32)[:, None, None]
        kj = np.arange(16)[None, :, None]
        ss = np.arange(128)[None, None, :]
        m["n_emat"] = (j == 2 * kj + ss // 64).astype(np.float32)
        cs = np.arange(128) * 16
        sl = np.arange(32) * 64
        ov = ((cs[:, None] < sl[None, :] + 64) & (cs[:, None] + 32 > sl[None, :]) & (np.arange(128)[:, None] < 127))
        m["n_ovl"] = ov.astype(np.float32)
        cur = (t // 64)[:, None]
        bid = np.arange(32)[None, :]
        fut = bid > cur
        forced = (bid == 0) | (bid == cur) | (bid == cur - 1)
        A = np.where(fut | forced, 0.0, 1.0).astype(np.float32)
        Bm = np.where(fut, -1e6, np.where(forced, 1e6, 0.0)).astype(np.float32)
        m["n_amk"] = np.ascontiguousarray(A.reshape(16, 128, 32).transpose(1, 0, 2))
        m["n_bmk"] = np.ascontiguousarray(Bm.reshape(16, 128, 32).transpose(1, 0, 2))
        for l in range(L):
            for kv, nm in enumerate(("k", "v")):
                m[f"n_w1_{l}_{kv}"] = np.ascontiguousarray(inp[f"nsa_cmp_{nm}_w1"][l].reshape(32, 128, 128).transpose(1, 0, 2))
            m[f"n_w2_{l}"] = np.ascontiguousarray(np.stack([inp["nsa_cmp_k_w2"][l], inp["nsa_cmp_v_w2"][l]], axis=1))
            m[f"n_pe_{l}"] = np.ascontiguousarray(np.stack([inp["nsa_cmp_pe_k"][l].T, inp["nsa_cmp_pe_v"][l].T], axis=1))
            for i, nm in enumerate(("a", "b", "c")):
                m[f"wp{l}_{i}"] = _colblock(inp[f"w_proj_{nm}"][l])
            m[f"wo{l}"] = _colblock(inp["w_out"][l])
    m["gcw"] = np.ascontiguousarray(inp["gdn_conv"].reshape(L, 4, 24, 128).transpose(3, 0, 2, 1))
    m["gdnn"] = np.ascontiguousarray(inp["gdn_out_norm"].T)
    m["hgz"] = np.ascontiguousarray(inp["hgrn_lb_logits"].reshape(L, 8, 128).transpose(2, 0, 1).reshape(128, L * 8))
    m["hgn"] = np.ascontiguousarray(inp["hgrn_out_norm"].reshape(L, 8, 128).transpose(2, 0, 1).reshape(128, L * 8))
    return m


_CACHE = {}


def run(inp, stages=None, dbg=None, n_cores=8, xT_override=None):
    nc = bass.Bass("TRN2", target_bir_lowering=False)
    kb = K(nc)
    kb.build(stages, dbg)
    need_ffn = stages is None or any(sg[0] == "ffn" for sg in stages)
    need_mix = stages is None or any(sg[0] != "ffn" for sg in stages)
    wm = prep_weights(inp, need_ffn, need_mix)
    x = inp["x"]
    in_maps = []
    for b in range(n_cores):
        d = dict(wm)
        if xT_override is not None:
            d["xT"] = np.ascontiguousarray(xT_override.reshape(NC16, 128, S))
        else:
            d["xT"] = np.ascontiguousarray(x[b].T.reshape(NC16, 128, S))
        in_maps.append(d)
    res = run_bass_kernel_spmd(nc, in_maps, core_ids=list(range(n_cores)))
    return res


def kernel(**inputs):
    inp = {k: np.asarray(v) for k, v in inputs.items()}
    res = run(inp)
    outs = [np.asarray(r["outT"]).reshape(D, S).T for r in res.results]
    return np.ascontiguousarray(np.stack(outs, axis=0)).astype(np.float32)
```
